# Optimizing a Trainium2 kernel written in Bass

```python
import math
import jax, jax.numpy as jnp
from jax import lax
import numpy as np

D_MODEL = 1024
BATCH = 4
SEQ = 8192
DEPTH = 4

GRID_W = 64
CTX_LEN = 256
N_MIXERS = 3
ALPHA = (2 * DEPTH) ** 0.25
BETA = (8 * DEPTH) ** -0.25
LN_EPS = 1e-5
SCAN_BLOCK = 128
S5_GROUP = 16
S5_GROUPS = D_MODEL // S5_GROUP
S5_STATE = 64
ML_INNER = 2 * D_MODEL
ML_HEADS = 4
ML_HEAD_DIM = ML_INNER // ML_HEADS
ML_CONV = 3
CV_KERNEL = 31
FFN_HIDDEN = ((8 * D_MODEL // 3 + 255) // 256) * 256
FFN_CONV = 3
N_S5_LAYERS = (DEPTH + 2) // 3
N_ML_LAYERS = (DEPTH + 1) // 3
N_CV_LAYERS = DEPTH // 3

kernel_name = 'hybrid_s5_mlstm_conformer_prefix_trunk'

F32 = jnp.float32


def layer_norm(x, g, b):
    xf = x.astype(F32)
    mu = xf.mean(-1, keepdims=True)
    var = jnp.square(xf - mu).mean(-1, keepdims=True)
    return ((xf - mu) * lax.rsqrt(var + LN_EPS) * g + b).astype(x.dtype)


def dwconv1d(x, w, b):
    k = w.shape[0]
    y = lax.conv_general_dilated(x, w[:, None, :].astype(x.dtype), (1,), [((k - 1) // 2, k // 2)],
                                 dimension_numbers=('NWC', 'WIO', 'NWC'), feature_group_count=x.shape[-1])
    return y + b


def dwconv2d(x, w, b):
    kh, kw = w.shape[:2]
    y = lax.conv_general_dilated(x, w[:, :, None, :].astype(x.dtype), (1, 1),
                                 [((kh - 1) // 2, kh // 2), ((kw - 1) // 2, kw // 2)],
                                 dimension_numbers=('NHWC', 'HWIO', 'NHWC'), feature_group_count=x.shape[-1])
    return y + b


def grid_transpose(x, rows, cols):
    bsz, _, ch = x.shape
    return x.reshape(bsz, rows, cols, ch).transpose(0, 2, 1, 3).reshape(bsz, rows * cols, ch)


def s5_discretise(lam_re, lam_im, log_dt, b_re, b_im):
    lre = jnp.minimum(lam_re.astype(F32), -1e-4)
    lim = lam_im.astype(F32)
    dt = jnp.exp(log_dt.astype(F32))[:, None]
    mag = jnp.exp(lre * dt)
    lb_re, lb_im = mag * jnp.cos(lim * dt), mag * jnp.sin(lim * dt)
    nr, ni = lb_re - 1.0, lb_im
    den = lre * lre + lim * lim
    cr = (nr * lre + ni * lim) / den
    ci = (ni * lre - nr * lim) / den
    br, bi = b_re.astype(F32), b_im.astype(F32)
    bb_re = cr[..., None] * br - ci[..., None] * bi
    bb_im = cr[..., None] * bi + ci[..., None] * br
    return lb_re, lb_im, bb_re, bb_im


def _cplx_affine_combine(e1, e2):
    a1r, a1i, b1r, b1i = e1
    a2r, a2i, b2r, b2i = e2
    return (a2r * a1r - a2i * a1i, a2r * a1i + a2i * a1r,
            a2r * b1r - a2i * b1i + b2r, a2r * b1i + a2i * b1r + b2i)


def s5_scan(u, lb_re, lb_im, bb_re, bb_im, c_re, c_im, s0):
    bsz, length, _ = u.shape
    nblk = length // SCAN_BLOCK
    ub = jnp.moveaxis(u.reshape(bsz, nblk, SCAN_BLOCK, S5_GROUPS, S5_GROUP), 1, 0)

    def step(carry, u_blk):
        sr, si = carry
        bu_re = jnp.einsum('btgs,gps->btgp', u_blk, bb_re)
        bu_im = jnp.einsum('btgs,gps->btgp', u_blk, bb_im)
        a_re = jnp.broadcast_to(lb_re, bu_re.shape)
        a_im = jnp.broadcast_to(lb_im, bu_im.shape)
        pa_re, pa_im, pb_re, pb_im = lax.associative_scan(_cplx_affine_combine, (a_re, a_im, bu_re, bu_im), axis=1)
        st_re = pa_re * sr[:, None] - pa_im * si[:, None] + pb_re
        st_im = pa_re * si[:, None] + pa_im * sr[:, None] + pb_im
        y = jnp.einsum('gsp,btgp->btgs', c_re, st_re) - jnp.einsum('gsp,btgp->btgs', c_im, st_im)
        return (st_re[:, -1], st_im[:, -1]), y.reshape(bsz, SCAN_BLOCK, D_MODEL)

    s_end, ys = lax.scan(step, s0, ub)
    return jnp.moveaxis(ys, 0, 1).reshape(bsz, length, D_MODEL), s_end


def s5_mixer(u_lat, u_ctx, lam_re, lam_im, log_dt, b_re, b_im, c_re, c_im, d_skip, w_glu, b_glu):
    ul, uc = u_lat.astype(F32), u_ctx.astype(F32)
    d = d_skip.astype(F32)
    y_lat, y_ctx = d * ul, d * uc
    bsz = ul.shape[0]
    for direction in range(2):
        lb_re, lb_im, bb_re, bb_im = s5_discretise(lam_re[direction], lam_im[direction], log_dt[direction],
                                                   b_re[direction], b_im[direction])
        cr, ci = c_re[direction].astype(F32), c_im[direction].astype(F32)
        zero = jnp.zeros((bsz, S5_GROUPS, S5_STATE), F32)
        rev = (lambda a: a[:, ::-1]) if direction == 1 else (lambda a: a)
        yc, s_ctx = s5_scan(rev(uc), lb_re, lb_im, bb_re, bb_im, cr, ci, (zero, zero))
        yl, _ = s5_scan(rev(ul), lb_re, lb_im, bb_re, bb_im, cr, ci, s_ctx)
        y_ctx = y_ctx + rev(yc)
        y_lat = y_lat + rev(yl)

    def glu(y):
        z = jax.nn.gelu(y).astype(u_lat.dtype) @ w_glu + b_glu
        val, gate = jnp.split(z, 2, axis=-1)
        return val * jax.nn.sigmoid(gate)

    return glu(y_lat), glu(y_ctx)


def mlstm_scan(q, k, v, ig, lf, state):
    bsz, nh, length, dh = q.shape
    nblk = length // SCAN_BLOCK
    blocks = lambda a: jnp.moveaxis(a.reshape(bsz, nh, nblk, SCAN_BLOCK, *a.shape[3:]), 2, 0)
    tri = jnp.tril(jnp.ones((SCAN_BLOCK, SCAN_BLOCK), bool))

    def step(carry, blk):
        C, n, m = carry
        qb, kb, vb, ib, fb = blk
        b = jnp.cumsum(fb, axis=-1)
        logd = jnp.where(tri, b[..., :, None] - b[..., None, :] + ib[..., None, :], -jnp.inf)
        m_inter = b + m[..., None]
        m_t = jnp.maximum(m_inter, logd.max(-1))
        w_inter = jnp.exp(m_inter - m_t)
        s = jnp.exp(logd - m_t[..., None]) * jnp.einsum('bhtd,bhsd->bhts', qb, kb)
        num = w_inter[..., None] * jnp.einsum('bhed,bhtd->bhte', C, qb) + jnp.einsum('bhts,bhse->bhte', s, vb)
        den = w_inter * jnp.einsum('bhd,bhtd->bht', n, qb) + s.sum(-1)
        h = num / jnp.maximum(jnp.abs(den), jnp.exp(-m_t))[..., None]
        decay = b[..., -1:] - b + ib
        m_new = jnp.maximum(b[..., -1] + m, decay.max(-1))
        w_prev = jnp.exp(b[..., -1] + m - m_new)
        w_r = jnp.exp(decay - m_new[..., None])
        C = w_prev[..., None, None] * C + jnp.einsum('bhse,bhsd->bhed', w_r[..., None] * vb, kb)
        n = w_prev[..., None] * n + jnp.einsum('bhs,bhsd->bhd', w_r, kb)
        return (C, n, m_new), h

    state, hs = lax.scan(step, state, tuple(blocks(a) for a in (q, k, v, ig, lf)))
    return jnp.moveaxis(hs, 0, 2).reshape(bsz, nh, length, dh), state


def mlstm_project(u, w_up, conv_w, conv_b, w_q, w_k, w_v, w_gates, b_gates):
    bsz, length, _ = u.shape
    xm, z = jnp.split(u @ w_up, 2, axis=-1)
    xc = jax.nn.silu(dwconv1d(xm, conv_w, conv_b))
    q = xc @ w_q
    k = (xc @ w_k) * ML_HEAD_DIM ** -0.5
    v = xm @ w_v
    g = q @ w_gates[0] + k @ w_gates[1] + v @ w_gates[2] + b_gates
    heads = lambda a: a.reshape(bsz, length, ML_HEADS, ML_HEAD_DIM).transpose(0, 2, 1, 3).astype(F32)
    return heads(q), heads(k), heads(v), g.reshape(bsz, length, 2, 2, ML_HEADS).astype(F32), xc, z


def mlstm_out(h, xc, z, gn_g, skip, w_down):
    bsz, nh, length, dh = h.shape
    mu = h.mean(-1, keepdims=True)
    var = jnp.square(h - mu).mean(-1, keepdims=True)
    hn = ((h - mu) * lax.rsqrt(var + LN_EPS)).transpose(0, 2, 1, 3).reshape(bsz, length, nh * dh)
    hn = (hn * gn_g).astype(xc.dtype) + skip * xc
    return (hn * jax.nn.silu(z)) @ w_down


def mlstm_mixer(u_lat, u_ctx, w_up, conv_w, conv_b, w_q, w_k, w_v, w_gates, b_gates, gn_g, skip, w_down):
    lat = mlstm_project(u_lat, w_up, conv_w, conv_b, w_q, w_k, w_v, w_gates, b_gates)
    cxt = mlstm_project(u_ctx, w_up, conv_w, conv_b, w_q, w_k, w_v, w_gates, b_gates)
    bsz = u_lat.shape[0]
    h_lat = jnp.zeros(lat[0].shape, F32)
    h_ctx = jnp.zeros(cxt[0].shape, F32)
    for direction in range(2):
        rev = (lambda a: jnp.flip(a, axis=2)) if direction == 1 else (lambda a: a)

        def prep(p):
            q, k, v, g = p[:4]
            ig = jnp.moveaxis(g[:, :, direction, 0], 1, 2)
            lf = jax.nn.log_sigmoid(jnp.moveaxis(g[:, :, direction, 1], 1, 2))
            return tuple(rev(a) for a in (q, k, v, ig, lf))

        st0 = (jnp.zeros((bsz, ML_HEADS, ML_HEAD_DIM, ML_HEAD_DIM), F32),
               jnp.zeros((bsz, ML_HEADS, ML_HEAD_DIM), F32),
               jnp.zeros((bsz, ML_HEADS), F32))
        hc, st_ctx = mlstm_scan(*prep(cxt), st0)
        hl, _ = mlstm_scan(*prep(lat), st_ctx)
        h_ctx = h_ctx + rev(hc)
        h_lat = h_lat + rev(hl)
    return (mlstm_out(h_lat, lat[4], lat[5], gn_g, skip, w_down),
            mlstm_out(h_ctx, cxt[4], cxt[5], gn_g, skip, w_down))


def conformer_conv(u, w_in, b_in, dw_w, dw_b, ln_g, ln_b, w_out, b_out):
    a, gate = jnp.split(u @ w_in + b_in, 2, axis=-1)
    hmid = a * jax.nn.sigmoid(gate)
    hmid = jax.nn.silu(layer_norm(dwconv1d(hmid, dw_w, dw_b), ln_g, ln_b))
    return hmid @ w_out + b_out


def conv_ffn(u, w_gate, w_up, conv_w, conv_b, w_down, rows, cols):
    bsz, length, _ = u.shape
    gate = dwconv2d((u @ w_gate).reshape(bsz, rows, cols, -1), conv_w, conv_b).reshape(bsz, length, -1)
    return (jax.nn.gelu(gate) * (u @ w_up)) @ w_down


def setup_inputs(seed: int = 0) -> dict:
    key = jax.random.key(seed)
    ks = iter(jax.random.split(key, 64))
    nrm = lambda shape, scale: scale * jax.random.normal(next(ks), shape, F32)
    D, F, E, G, P, GS, H = D_MODEL, FFN_HIDDEN, ML_INNER, S5_GROUPS, S5_STATE, S5_GROUP, ML_HEADS
    NA, NB, NC = N_S5_LAYERS, N_ML_LAYERS, N_CV_LAYERS
    inp = {}
    inp['x'] = nrm((BATCH, SEQ, D), 1.0)
    inp['c'] = nrm((BATCH, D), 1.0)
    inp['ctx'] = nrm((BATCH, CTX_LEN, D), 1.0)
    inp['c_ctx'] = nrm((D,), 1.0)
    inp['mod_w'] = nrm((DEPTH, D, 6 * D), 0.2 * D ** -0.5)
    inp['mod_b'] = nrm((DEPTH, 6 * D), 0.02)
    inp['post_ln_g'] = 1.0 + nrm((DEPTH, 2, D), 0.02)
    inp['post_ln_b'] = nrm((DEPTH, 2, D), 0.02)
    inp['ffn_w_gate'] = nrm((DEPTH, D, F), D ** -0.5)
    inp['ffn_w_up'] = nrm((DEPTH, D, F), D ** -0.5)
    inp['ffn_conv_w'] = nrm((DEPTH, FFN_CONV, FFN_CONV, F), 1.0 / FFN_CONV)
    inp['ffn_conv_b'] = nrm((DEPTH, F), 0.02)
    inp['ffn_w_down'] = nrm((DEPTH, F, D), BETA * F ** -0.5)
    inp['s5_lambda_re'] = -0.5 + nrm((NA, 2, G, P), 0.01)
    inp['s5_lambda_im'] = math.pi * jnp.arange(P, dtype=F32) + nrm((NA, 2, G, P), 0.01)
    inp['s5_log_dt'] = jax.random.uniform(next(ks), (NA, 2, G), F32, math.log(1e-3), math.log(1e-1))
    inp['s5_b_re'] = nrm((NA, 2, G, P, GS), (2 * GS) ** -0.5)
    inp['s5_b_im'] = nrm((NA, 2, G, P, GS), (2 * GS) ** -0.5)
    inp['s5_c_re'] = nrm((NA, 2, G, GS, P), (2 * P) ** -0.5)
    inp['s5_c_im'] = nrm((NA, 2, G, GS, P), (2 * P) ** -0.5)
    inp['s5_d'] = nrm((NA, D), 1.0)
    inp['s5_w_glu'] = jnp.concatenate([nrm((NA, D, D), BETA * D ** -0.5), nrm((NA, D, D), D ** -0.5)], axis=-1)
    inp['s5_b_glu'] = nrm((NA, 2 * D), 0.01)
    inp['ml_w_up'] = nrm((NB, D, 2 * E), D ** -0.5)
    inp['ml_conv_w'] = nrm((NB, ML_CONV, E), ML_CONV ** -0.5)
    inp['ml_conv_b'] = nrm((NB, E), 0.02)
    inp['ml_w_q'] = nrm((NB, E, E), E ** -0.5)
    inp['ml_w_k'] = nrm((NB, E, E), E ** -0.5)
    inp['ml_w_v'] = nrm((NB, E, E), E ** -0.5)
    inp['ml_w_gates'] = nrm((NB, 3, E, 4 * H), 0.1 * (3 * E) ** -0.5)
    ig_b = nrm((NB, 2, 1, H), 0.1)
    fg_b = jnp.linspace(3.0, 6.0, H, dtype=F32) + nrm((NB, 2, 1, H), 0.1)
    inp['ml_b_gates'] = jnp.concatenate([ig_b, fg_b], axis=2).reshape(NB, 4 * H)
    inp['ml_gn_g'] = 1.0 + nrm((NB, E), 0.02)
    inp['ml_skip'] = 1.0 + nrm((NB, E), 0.02)
    inp['ml_w_down'] = nrm((NB, E, D), BETA * E ** -0.5)
    inp['cv_w_in'] = nrm((NC, D, 2 * D), D ** -0.5)
    inp['cv_b_in'] = nrm((NC, 2 * D), 0.02)
    inp['cv_dw_w'] = nrm((NC, CV_KERNEL, D), CV_KERNEL ** -0.5)
    inp['cv_dw_b'] = nrm((NC, D), 0.02)
    inp['cv_ln_g'] = 1.0 + nrm((NC, D), 0.02)
    inp['cv_ln_b'] = nrm((NC, D), 0.02)
    inp['cv_w_out'] = nrm((NC, D, D), BETA * D ** -0.5)
    inp['cv_b_out'] = nrm((NC, D), 0.02)
    return inp


def reference(x, c, ctx, c_ctx, mod_w, mod_b, post_ln_g, post_ln_b,
              ffn_w_gate, ffn_w_up, ffn_conv_w, ffn_conv_b, ffn_w_down,
              s5_lambda_re, s5_lambda_im, s5_log_dt, s5_b_re, s5_b_im, s5_c_re, s5_c_im, s5_d, s5_w_glu, s5_b_glu,
              ml_w_up, ml_conv_w, ml_conv_b, ml_w_q, ml_w_k, ml_w_v, ml_w_gates, ml_b_gates, ml_gn_g, ml_skip, ml_w_down,
              cv_w_in, cv_b_in, cv_dw_w, cv_dw_b, cv_ln_g, cv_ln_b, cv_w_out, cv_b_out):
    length = x.shape[1]
    rows = length // GRID_W
    ctx_len = ctx.shape[1]
    silu_c = jax.nn.silu(c)[:, None, :]
    silu_cc = jax.nn.silu(c_ctx)[None, None, :]
    h_lat, h_ctx = x, ctx
    for i in range(DEPTH):
        kind, occ = i % N_MIXERS, i // N_MIXERS
        last = i == DEPTH - 1
        sh1, sc1, g1, sh2, sc2, g2 = jnp.split(silu_c @ mod_w[i] + mod_b[i], 6, axis=-1)
        csh1, csc1, cg1, csh2, csc2, cg2 = jnp.split(silu_cc @ mod_w[i] + mod_b[i], 6, axis=-1)
        u_lat = h_lat * (1 + sc1) + sh1
        u_ctx = h_ctx * (1 + csc1) + csh1
        col_major = (kind != 2) and (occ % 2 == 1)
        if col_major:
            u_lat = grid_transpose(u_lat, rows, GRID_W)
        if kind == 0:
            y_lat, y_ctx = s5_mixer(u_lat, u_ctx, s5_lambda_re[occ], s5_lambda_im[occ], s5_log_dt[occ],
                                    s5_b_re[occ], s5_b_im[occ], s5_c_re[occ], s5_c_im[occ], s5_d[occ],
                                    s5_w_glu[occ], s5_b_glu[occ])
        elif kind == 1:
            y_lat, y_ctx = mlstm_mixer(u_lat, u_ctx, ml_w_up[occ], ml_conv_w[occ], ml_conv_b[occ], ml_w_q[occ],
                                       ml_w_k[occ], ml_w_v[occ], ml_w_gates[occ], ml_b_gates[occ], ml_gn_g[occ],
                                       ml_skip[occ], ml_w_down[occ])
        else:
            y_lat = conformer_conv(u_lat, cv_w_in[occ], cv_b_in[occ], cv_dw_w[occ], cv_dw_b[occ],
                                   cv_ln_g[occ], cv_ln_b[occ], cv_w_out[occ], cv_b_out[occ])
            y_ctx = conformer_conv(u_ctx, cv_w_in[occ], cv_b_in[occ], cv_dw_w[occ], cv_dw_b[occ],
                                   cv_ln_g[occ], cv_ln_b[occ], cv_w_out[occ], cv_b_out[occ])
        if col_major:
            y_lat = grid_transpose(y_lat, GRID_W, rows)
        h_lat = layer_norm(ALPHA * h_lat + (1 + g1) * y_lat, post_ln_g[i, 0], post_ln_b[i, 0])
        f_lat = conv_ffn(h_lat * (1 + sc2) + sh2, ffn_w_gate[i], ffn_w_up[i], ffn_conv_w[i], ffn_conv_b[i],
                         ffn_w_down[i], rows, GRID_W)
        h_lat = layer_norm(ALPHA * h_lat + (1 + g2) * f_lat, post_ln_g[i, 1], post_ln_b[i, 1])
        if not last:
            h_ctx = layer_norm(ALPHA * h_ctx + (1 + cg1) * y_ctx, post_ln_g[i, 0], post_ln_b[i, 0])
            f_ctx = conv_ffn(h_ctx * (1 + csc2) + csh2, ffn_w_gate[i], ffn_w_up[i], ffn_conv_w[i], ffn_conv_b[i],
                             ffn_w_down[i], 1, ctx_len)
            h_ctx = layer_norm(ALPHA * h_ctx + (1 + cg2) * f_ctx, post_ln_g[i, 1], post_ln_b[i, 1])
    return h_lat
```

```python
import contextlib
import numpy as np
import concourse.bass as bass
import concourse.mybir as mybir
from concourse.bass_utils import run_bass_kernel_spmd

F32 = mybir.dt.float32
BF16 = mybir.dt.bfloat16
AF = mybir.ActivationFunctionType
ALU = mybir.AluOpType

D = 1024
B = 4
L = 8192
DEPTH = 4
GW = 64
CTX = 256
ALPHA = (2 * DEPTH) ** 0.25
EPS = 1e-5
FF = 2816
NF = FF // 128
NCORES = 8


class Buf:
    def __init__(self, t, track=True, is_out=False):
        self.t = t
        self.w = None
        self.r = {}
        self.track = track
        self.is_out = is_out

    def __getitem__(self, idx):
        return self.t[idx]


class KB:
    def __init__(self):
        self.nc = bass.Bass("TRN2", target_bir_lowering=False)
        self.es = contextlib.ExitStack()
        nc = self.nc
        self.engs = {}
        for name, e in [("pe", nc.tensor), ("act", nc.scalar), ("dve", nc.vector),
                        ("pool", nc.gpsimd), ("sp", nc.sync)]:
            sem = self.es.enter_context(nc.semaphore("sem_" + name))
            self.engs[name] = dict(e=e, sem=sem, n=0, waited={}, name=name)
        self.dsems = [self.es.enter_context(nc.semaphore("dsem%d" % i)) for i in range(16)]
        self.dval = [0] * 16
        self.di = 0
        self.out_tok = []
        self.nps = 0

    def sb(self, name, shape, dt=F32):
        self.nps += 1
        return Buf(self.es.enter_context(self.nc.sbuf_tensor("%s_%d" % (name, self.nps), list(shape), dt)))

    def ps(self, name, shape=(128, 512), dt=F32):
        self.nps += 1
        return Buf(self.es.enter_context(self.nc.psum_tensor("%s_%d" % (name, self.nps), list(shape), dt)))

    def din(self, name, shape, dt=F32):
        return self.nc.dram_tensor(name, list(shape), dt, kind="ExternalInput").ap()

    def dout(self, name, shape, dt=F32):
        return self.nc.dram_tensor(name, list(shape), dt, kind="ExternalOutput").ap()

    def _wait(self, E, deps):
        for tok in deps:
            if tok is None:
                continue
            sem, val, src = tok
            if src == "pe" and E["name"] == "pe":
                continue
            key = id(sem)
            if E["waited"].get(key, 0) < val:
                E["e"].wait_ge(sem, val)
                E["waited"][key] = val

    def _deps(self, reads, writes):
        deps = []
        for b in reads:
            if b.track:
                deps.append(b.w)
        for b in writes:
            if b.track:
                deps.append(b.w)
                deps.extend(b.r.values())
        return deps

    def dram(self, name, shape, kind="Internal", dt=F32):
        t = self.nc.dram_tensor(name, list(shape), dt, kind=kind).ap()
        return Buf(t, track=False, is_out=(kind == "ExternalOutput"))

    def op(self, eng, fn, reads=(), writes=()):
        E = self.engs[eng]
        self._wait(E, self._deps(reads, writes))
        ins = fn(E["e"])
        E["n"] += 1
        ins.then_inc(E["sem"], 1)
        tok = (E["sem"], E["n"], eng)
        for b in writes:
            if b.track:
                b.w = tok
                b.r = {}
        for b in reads:
            if b.track:
                b.r[eng] = tok
        return ins

    def dma(self, out, in_, reads=(), writes=(), q="sp", is_out=False):
        E = self.engs[q]
        self._wait(E, self._deps(reads, writes))
        i = self.di
        self.di = (self.di + 1) % len(self.dsems)
        self.dval[i] += 16
        E["e"].dma_start(out=out, in_=in_).then_inc(self.dsems[i], 16)
        tok = (self.dsems[i], self.dval[i], "dma%d" % i)
        for b in writes:
            if b.track:
                b.w = tok
                b.r = {}
        for b in reads:
            if b.track:
                b.r["dma%d_%d" % (i, self.dval[i])] = tok
        if is_out:
            self.out_tok.append(tok)

    def finish(self):
        barrier(self)
        self.es.close()
        return self.nc


def mm(kb, ps, lhsT, rhs, start, stop, reads, ps_ap=None):
    out = ps_ap if ps_ap is not None else ps.t[:]
    kb.op("pe", lambda e: e.matmul(out, lhsT, rhs, start=start, stop=stop), reads=reads, writes=[ps])


def barrier(kb):
    toks = [(E["sem"], E["n"], "bar") for E in kb.engs.values() if E["n"] > 0]
    toks += [(kb.dsems[i], kb.dval[i], "bar") for i in range(len(kb.dsems)) if kb.dval[i] > 0]
    for E in kb.engs.values():
        kb._wait(E, toks)


class Phase:
    def __init__(self, kb):
        self.kb = kb

    def __enter__(self):
        self.outer = self.kb.es
        self.kb.es = contextlib.ExitStack()
        return self

    def __exit__(self, *a):
        barrier(self.kb)
        self.kb.es.close()
        self.kb.es = self.outer


def fm(vec):
    return np.ascontiguousarray(vec.reshape(-1, 128).T)


def to_fm(a):
    t, d = a.shape
    return np.ascontiguousarray(a.T.reshape(d // 128, 128, t).transpose(1, 0, 2))


def from_fm(a):
    p, k, t = a.shape
    return np.ascontiguousarray(a.transpose(1, 0, 2).reshape(k * 128, t).T)


def ts(kb, eng, out, in0, s1, s2, op0, op1, reads, writes):
    if s2 is None:
        kb.op(eng, lambda e: e.tensor_scalar(out=out, in0=in0, scalar1=s1, scalar2=None, op0=op0), reads, writes)
    else:
        kb.op(eng, lambda e: e.tensor_scalar(out=out, in0=in0, scalar1=s1, scalar2=s2, op0=op0, op1=op1), reads, writes)


def stt(kb, out, in0, sc, in1, op0, op1, reads, writes):
    kb.op("dve", lambda e: e.scalar_tensor_tensor(out=out, in0=in0, scalar=sc, in1=in1, op0=op0, op1=op1), reads, writes)


def tt(kb, out, in0, in1, op, reads, writes, eng="dve"):
    kb.op(eng, lambda e: e.tensor_tensor(out=out, in0=in0, in1=in1, op=op), reads, writes)


def act(kb, out, in_, func, reads, writes, bias=None, scale=None):
    kw = {}
    if bias is not None:
        kw["bias"] = bias
    if scale is not None:
        kw["scale"] = scale
    kb.op("act", lambda e: e.activation(out=out, in_=in_, func=func, **kw), reads, writes)


class LNHelper:
    def __init__(self, kb, nmax):
        self.kb = kb
        self.ones = kb.sb("ln_ones", [128, 128])
        self.sq = kb.sb("ln_sq", [128, nmax])
        self.mean = kb.sb("ln_mean", [128, nmax])
        self.m2 = kb.sb("ln_m2", [128, nmax])
        self.rstd = kb.sb("ln_rstd", [128, nmax])
        self.tmp = kb.sb("ln_tmp", [128, nmax])
        self.S1 = kb.ps("ln_S1")
        self.S2 = kb.ps("ln_S2")
        kb.op("dve", lambda e: e.memset(self.ones.t[:], 1.0), writes=[self.ones])

    def __call__(self, src, lo, n, dst, dlo, gvec, bvec, vbuf, func=None):
        kb = self.kb
        sq, mean, m2, rstd, tmp, S1, S2, ones = self.sq, self.mean, self.m2, self.rstd, self.tmp, self.S1, self.S2, self.ones
        for k in range(8):
            act(kb, sq.t[:, :n], src.t[:, k, lo:lo + n], AF.Square, [src], [sq])
            mm(kb, S1, ones.t[:], src.t[:, k, lo:lo + n], k == 0, k == 7, [ones, src], ps_ap=S1.t[:, :n])
            mm(kb, S2, ones.t[:], sq.t[:, :n], k == 0, k == 7, [ones, sq], ps_ap=S2.t[:, :n])
        kb.op("act", lambda e: e.mul(mean.t[:, :n], S1.t[:, :n], 1.0 / D), reads=[S1], writes=[mean])
        tt(kb, m2.t[:, :n], mean.t[:, :n], mean.t[:, :n], ALU.mult, [mean], [m2])
        stt(kb, m2.t[:, :n], S2.t[:, :n], 1.0 / D, m2.t[:, :n], ALU.mult, ALU.subtract, [S2, m2], [m2])
        ts(kb, "dve", m2.t[:, :n], m2.t[:, :n], EPS, None, ALU.add, None, [m2], [m2])
        act(kb, m2.t[:, :n], m2.t[:, :n], AF.Sqrt, [m2], [m2])
        kb.op("dve", lambda e: e.reciprocal(out=rstd.t[:, :n], in_=m2.t[:, :n]), reads=[m2], writes=[rstd])
        for k in range(8):
            tt(kb, tmp.t[:, :n], src.t[:, k, lo:lo + n], mean.t[:, :n], ALU.subtract, [src, mean], [tmp])
            tt(kb, tmp.t[:, :n], tmp.t[:, :n], rstd.t[:, :n], ALU.mult, [tmp, rstd], [tmp])
            if func is None:
                ts(kb, "dve", dst.t[:, k, dlo:dlo + n], tmp.t[:, :n], gvec(k), bvec(k), ALU.mult, ALU.add,
                   [tmp, vbuf], [dst])
            else:
                ts(kb, "dve", tmp.t[:, :n], tmp.t[:, :n], gvec(k), bvec(k), ALU.mult, ALU.add, [tmp, vbuf], [tmp])
                act(kb, dst.t[:, k, dlo:dlo + n], tmp.t[:, :n], func, [tmp], [dst])


def phase_mod(kb, MV, cT, modw, modb):
    with Phase(kb):
        cb = kb.sb("m_cb", [128, 8, 2])
        sc = kb.sb("m_sc", [128, 8, 2])
        bb = kb.sb("m_bb", [128, DEPTH, 48])
        wb = [kb.sb("m_wb%d" % i, [128, 8, 512]) for i in range(2)]
        pss = [kb.ps("m_ps%d" % i) for i in range(2)]
        kb.dma(cb.t[:], cT, writes=[cb])
        kb.dma(bb.t[:], modb, writes=[bb])
        act(kb, sc.t[:], cb.t[:], AF.Silu, [cb], [sc])
        n = 0
        for l in range(DEPTH):
            for c in range(12):
                w = wb[n % 2]
                kb.dma(w.t[:], modw[l, :, :, c * 512:(c + 1) * 512], writes=[w])
                for f4 in range(4):
                    ft = c * 4 + f4
                    ps = pss[ft % 2]
                    for k in range(8):
                        mm(kb, ps, w.t[:, k, f4 * 128:(f4 + 1) * 128], sc.t[:, k, :], k == 0, k == 7, [w, sc],
                           ps_ap=ps.t[:, 0:2])
                    j = ft // 8
                    ts(kb, "dve", MV.t[:, l, ft, :], ps.t[:, 0:2], bb.t[:, l, ft:ft + 1],
                       1.0 if j in (1, 2, 4, 5) else 0.0, ALU.add, ALU.add, [ps, bb], [MV])
                n += 1


NV_P2 = 32 + NF + NF * 9


def phase_p2(kb, l, MV, H, Y, HO, p2vec, wg, wu, wd, Lx, Cx):
    rows = Lx // GW
    RT = 8
    with Phase(kb):
        vb = kb.sb("p_vb", [128, NV_P2])
        N1M = (RT + 2) * GW
        hb = kb.sb("p_hb", [128, 8, N1M])
        yb = kb.sb("p_yb", [128, 8, N1M])
        u2 = kb.sb("p_u2", [128, 8, N1M], BF16)
        tmp = kb.sb("p_tmp", [128, N1M])
        hid = kb.sb("p_hid", [128, NF, 512], BF16)
        wgb = [kb.sb("p_wgb%d" % i, [128, 8, 128], BF16) for i in range(4)]
        wub = [kb.sb("p_wub%d" % i, [128, 8, 128], BF16) for i in range(4)]
        wdb = [kb.sb("p_wdb%d" % i, [128, NF, 128], BF16) for i in range(3)]
        gpads = [kb.sb("p_gpad%d" % i, [128, RT + 2, 66]) for i in range(2)]
        gpcs = [kb.sb("p_gpc%d" % i, [128, 514]) for i in range(2)]
        accs = [kb.sb("p_acc%d" % i, [128, 512]) for i in range(2)]
        accb = kb.sb("p_accb", [128, 512])
        hgs = [kb.sb("p_hg%d" % i, [128, 512]) for i in range(2)]
        z2 = kb.sb("p_z2", [128, 8, 512])
        ln = LNHelper(kb, 512)
        GA = kb.ps("p_GA")
        GB = kb.ps("p_GB")
        Up = [kb.ps("p_Up%d" % i) for i in range(2)]
        Fp = [kb.ps("p_Fp%d" % i) for i in range(2)]
        kb.dma(vb.t[:], p2vec[l], writes=[vb])
        WGb = kb.dram("p2wgb_%d" % l, [NF, 128, 8, 128], dt=BF16)
        WUb = kb.dram("p2wub_%d" % l, [NF, 128, 8, 128], dt=BF16)
        WDb = kb.dram("p2wdb_%d" % l, [8, 128, NF, 128], dt=BF16)
        for f in range(NF):
            for src_, dst_, stg in ((wg, WGb, wgb), (wu, WUb, wub)):
                kb.dma(stg[f % 2].t[:], src_[l, f], writes=[stg[f % 2]], q="pool")
                kb.dma(dst_.t[f], stg[f % 2].t[:], reads=[stg[f % 2]])
        for j in range(8):
            kb.dma(wdb[j % 2].t[:], wd[l, j], writes=[wdb[j % 2]], q="pool")
            kb.dma(WDb.t[j], wdb[j % 2].t[:], reads=[wdb[j % 2]])
        barrier(kb)
        for g_ in gpads + gpcs:
            kb.op("dve", lambda e: e.memset(g_.t[:], 0.0), writes=[g_])
        CB0, CW0 = 32, 32 + NF

        def vcol(i):
            return vb.t[:, i:i + 1]

        def mv(j, k, s):
            return MV.t[:, l, j * 8 + k, s:s + 1]

        wcount = [0]

        def tile(t0, n1, i0, n2, s, gp0, zero_rows):
            is_ctx = s == 1
            kb.dma(hb.t[:, :, :n1], H.t[:, :, t0:t0 + n1], writes=[hb])
            kb.dma(yb.t[:, :, :n1], Y.t[:, :, t0:t0 + n1], writes=[yb])
            for k in range(8):
                ts(kb, "dve", tmp.t[:, :n1], yb.t[:, k, :n1], mv(2, k, s), None, ALU.mult, None, [yb, MV], [tmp])
                stt(kb, yb.t[:, k, :n1], hb.t[:, k, :n1], ALPHA, tmp.t[:, :n1], ALU.mult, ALU.add, [hb, tmp], [yb])
            for c0 in range(0, n1, 512):
                cn = min(512, n1 - c0)
                ln(yb, c0, cn, hb, c0, lambda k: vcol(k), lambda k: vcol(8 + k), vb)
            for k in range(8):
                ts(kb, "dve", u2.t[:, k, :n1], hb.t[:, k, :n1], mv(4, k, s), mv(3, k, s), ALU.mult, ALU.add,
                   [hb, MV], [u2])
            for zr in zero_rows:
                for g_ in gpads:
                    kb.op("dve", lambda e: e.memset(g_.t[:, zr, :], 0.0), writes=[g_])
            na = min(n1, 512)
            nb_ = n1 - na

            def load_gu(f):
                kb.dma(wgb[f % 4].t[:], WGb.t[f], writes=[wgb[f % 4]], q="pool")
                kb.dma(wub[f % 4].t[:], WUb.t[f], writes=[wub[f % 4]], q="pool")

            def load_d(j):
                kb.dma(wdb[j % 3].t[:], WDb.t[j], writes=[wdb[j % 3]], q="pool")

            def stage_a(f):
                sl = f % 2
                wq_, wu_ = wgb[f % 4], wub[f % 4]
                U = Up[sl]
                gp_ = gpads[sl]
                for k in range(8):
                    mm(kb, GA, wq_.t[:, k, :], u2.t[:, k, :na], k == 0, k == 7, [wq_, u2], ps_ap=GA.t[:, :na])
                if nb_ > 0:
                    for k in range(8):
                        mm(kb, GB, wq_.t[:, k, :], u2.t[:, k, na:n1], k == 0, k == 7, [wq_, u2], ps_ap=GB.t[:, :nb_])
                for k in range(8):
                    mm(kb, U, wu_.t[:, k, :], u2.t[:, k, i0:i0 + n2], k == 0, k == 7, [wu_, u2],
                       ps_ap=U.t[:, :n2])
                if not is_ctx:
                    ra = na // GW
                    kb.op("act", lambda e: e.copy(out=gp_.t[:, gp0:gp0 + ra, 1:65],
                                                  in_=GA.t[:, :na].rearrange("p (r c) -> p r c", c=GW)),
                          reads=[GA], writes=[gp_])
                    if nb_ > 0:
                        rb = nb_ // GW
                        kb.op("act", lambda e: e.copy(out=gp_.t[:, gp0 + ra:gp0 + ra + rb, 1:65],
                                                      in_=GB.t[:, :nb_].rearrange("p (r c) -> p r c", c=GW)),
                              reads=[GB], writes=[gp_])
                else:
                    kb.op("act", lambda e: e.copy(out=gpcs[sl].t[:, 1:1 + n1], in_=GA.t[:, :n1]), reads=[GA], writes=[gpcs[sl]])

            def stage_b(f):
                sl = f % 2
                cw = CW0 + f * 9
                ac, hg_ = accs[sl], hgs[sl]
                acb = accb
                if not is_ctx:
                    gp_ = gpads[sl]
                    a3 = ac.t[:, :n2].rearrange("p (r c) -> p r c", c=GW)
                    b3 = acb.t[:, :n2].rearrange("p (r c) -> p r c", c=GW)
                    taps = [(kh, kw) for kh in range(3) for kw in range(3)]
                    for ti_, (kh, kw) in enumerate(taps):
                        src = gp_.t[:, kh:kh + RT, kw:kw + 64]
                        wc = vcol(cw + kh * 3 + kw)
                        dst3, dbuf = (a3, ac) if ti_ % 2 == 0 else (b3, acb)
                        if ti_ < 2:
                            ts(kb, "dve", dst3, src, wc, None, ALU.mult, None, [gp_, vb], [dbuf])
                        else:
                            stt(kb, dst3, src, wc, dst3, ALU.mult, ALU.add, [gp_, vb, dbuf], [dbuf])
                    tt(kb, ac.t[:, :n2], ac.t[:, :n2], acb.t[:, :n2], ALU.add, [ac, acb], [ac])
                else:
                    gq = gpcs[sl]
                    for kw in range(3):
                        src = gq.t[:, kw:kw + n1]
                        wc = vcol(cw + 3 + kw)
                        if kw == 0:
                            ts(kb, "dve", ac.t[:, :n1], src, wc, None, ALU.mult, None, [gq, vb], [ac])
                        else:
                            stt(kb, ac.t[:, :n1], src, wc, ac.t[:, :n1], ALU.mult, ALU.add, [gq, vb, ac], [ac])
                act(kb, hg_.t[:, :n2], ac.t[:, :n2], AF.Gelu, [ac, vb], [hg_], bias=vcol(CB0 + f))

            def stage_c(f):
                sl = f % 2
                tt(kb, hid.t[:, f, :n2], hgs[sl].t[:, :n2], Up[sl].t[:, :n2], ALU.mult, [hgs[sl], Up[sl]], [hid])

            load_gu(0)
            load_gu(1)
            load_gu(2)
            load_d(0)
            load_d(1)
            stage_a(0)
            stage_a(1)
            stage_b(0)
            for f in range(NF):
                if f + 3 < NF:
                    load_gu(f + 3)
                if f + 1 < NF:
                    stage_b(f + 1)
                stage_c(f)
                if f + 2 < NF:
                    stage_a(f + 2)
            for j in range(8):
                sl = j % 2
                if j + 2 < 8:
                    load_d(j + 2)
                wd_ = wdb[j % 3]
                Fq = Fp[sl]
                for f in range(NF):
                    mm(kb, Fq, wd_.t[:, f, :], hid.t[:, f, :n2], f == 0, f == NF - 1, [wd_, hid],
                       ps_ap=Fq.t[:, :n2])
                ts(kb, "dve", tmp.t[:, :n2], Fq.t[:, :n2], mv(5, j, s), None, ALU.mult, None, [Fq, MV], [tmp])
                stt(kb, z2.t[:, j, :n2], hb.t[:, j, i0:i0 + n2], ALPHA, tmp.t[:, :n2], ALU.mult, ALU.add,
                    [hb, tmp], [z2])
            ln(z2, 0, n2, z2, 0, lambda k: vcol(16 + k), lambda k: vcol(24 + k), vb)
            kb.dma(HO.t[:, :, t0 + i0:t0 + i0 + n2], z2.t[:, :, :n2], reads=[z2])

        ntile = rows // RT
        for ti in range(ntile):
            rlo = max(RT * ti - 1, 0)
            rhi = min(RT * ti + RT + 1, rows)
            t0 = Cx + rlo * GW
            n1 = (rhi - rlo) * GW
            i0 = (RT * ti - rlo) * GW
            gp0 = rlo - (RT * ti - 1)
            zr = ([0] if ti == 0 else []) + ([RT + 1] if ti == ntile - 1 else [])
            tile(t0, n1, i0, RT * GW, 0, gp0, zr)
        for c0 in range(0, Cx, 512):
            assert Cx <= 512
        tile(0, Cx, 0, Cx, 1, 0, [])


NV_CV = 16 + 31 * 8 + 8 + 8 + 8 + 8


def phase_conf(kb, l, MV, H, Y, cvvec, cv_win, cv_wout, ident, Lx, Cx):
    with Phase(kb):
        vb = kb.sb("c_vb", [128, NV_CV])
        win = kb.sb("c_win", [128, 8, 2048], BF16)
        wout = kb.sb("c_wout", [128, 8, 1024], BF16)
        NW = 286
        hb = kb.sb("c_hb", [128, 8, NW])
        ub = kb.sb("c_ub", [128, 8, NW], BF16)
        sg = kb.sb("c_sg", [128, NW])
        hp = kb.sb("c_hp", [128, 8, NW], BF16)
        idn = kb.sb("c_idn", [128, 128])
        dg = kb.sb("c_dg", [128, 248, 128], BF16)
        Cp = [kb.ps("c_Cp%d" % i) for i in range(2)]
        cv = kb.sb("c_cv", [128, 8, 256])
        hm = kb.sb("c_hm", [128, 8, 256], BF16)
        ob = kb.sb("c_ob", [128, 8, 256])
        ln = LNHelper(kb, 256)
        Ap = [kb.ps("c_Ap%d" % i) for i in range(2)]
        Bp = [kb.ps("c_Bp%d" % i) for i in range(2)]
        kb.dma(vb.t[:], cvvec, writes=[vb])
        for k in range(8):
            kb.dma(win.t[:, k, :], cv_win[:, k, :], writes=[win], q="pool")
            kb.dma(wout.t[:, k, :], cv_wout[:, k, :], writes=[wout], q="pool")
        BI0, DW0, DB0, LG0, LB0, BO0 = 0, 16, 16 + 248, 16 + 256, 16 + 264, 16 + 272
        kb.dma(idn.t[:], ident, writes=[idn])
        for i in range(248):
            ts(kb, "dve", dg.t[:, i, :], idn.t[:], vb.t[:, DW0 + i:DW0 + i + 1], None, ALU.mult, None, [idn, vb], [dg])

        def vcol(i):
            return vb.t[:, i:i + 1]

        def tile(seq0, seqn, p0, n2, s):
            lo = max(p0 - 15, 0)
            hi = min(p0 + n2 + 15, seqn)
            n1 = hi - lo
            off = lo - (p0 - 15)
            kb.dma(hb.t[:, :, :n1], H.t[:, :, seq0 + lo:seq0 + hi], reads=[H], writes=[hb])
            if n1 < n2 + 30:
                kb.op("dve", lambda e: e.memset(hp.t[:], 0.0), writes=[hp])
            for k in range(8):
                ts(kb, "dve", ub.t[:, k, :n1], hb.t[:, k, :n1], MV.t[:, l, 8 + k, s:s + 1], MV.t[:, l, k, s:s + 1],
                   ALU.mult, ALU.add, [hb, MV], [ub])
            for j in range(8):
                A, Bq = Ap[j % 2], Bp[j % 2]
                for k in range(8):
                    mm(kb, A, win.t[:, k, j * 128:(j + 1) * 128], ub.t[:, k, :n1], k == 0, k == 7, [win, ub],
                       ps_ap=A.t[:, :n1])
                for k in range(8):
                    mm(kb, Bq, win.t[:, k, 1024 + j * 128:1024 + (j + 1) * 128], ub.t[:, k, :n1], k == 0, k == 7,
                       [win, ub], ps_ap=Bq.t[:, :n1])
                act(kb, sg.t[:, :n1], Bq.t[:, :n1], AF.Sigmoid, [Bq, vb], [sg], bias=vcol(BI0 + 8 + j))
                stt(kb, hp.t[:, j, off:off + n1], A.t[:, :n1], vcol(BI0 + j), sg.t[:, :n1], ALU.add, ALU.mult,
                    [A, vb, sg], [hp])
            for k in range(8):
                Cq = Cp[k % 2]
                for tap in range(31):
                    mm(kb, Cq, dg.t[:, tap * 8 + k, :], hp.t[:, k, tap:tap + n2], tap == 0, tap == 30, [dg, hp],
                       ps_ap=Cq.t[:, :n2])
                ts(kb, "dve", cv.t[:, k, :n2], Cq.t[:, :n2], vcol(DB0 + k), None, ALU.add, None, [Cq, vb], [cv])
            ln(cv, 0, n2, hm, 0, lambda k: vcol(LG0 + k), lambda k: vcol(LB0 + k), vb, func=AF.Silu)
            for j in range(8):
                A = Ap[j % 2]
                for k in range(8):
                    mm(kb, A, wout.t[:, k, j * 128:(j + 1) * 128], hm.t[:, k, :n2], k == 0, k == 7, [wout, hm],
                       ps_ap=A.t[:, :n2])
                ts(kb, "dve", ob.t[:, j, :n2], A.t[:, :n2], vcol(BO0 + j), None, ALU.add, None, [A, vb], [ob])
            kb.dma(Y.t[:, :, seq0 + p0:seq0 + p0 + n2], ob.t[:, :, :n2], reads=[ob], writes=[Y])

        for p0 in range(0, Lx, 256):
            tile(Cx, Lx, p0, 256, 0)
        for p0 in range(0, Cx, 256):
            tile(0, Cx, p0, min(256, Cx - p0), 1)


I32 = mybir.dt.int32
PI = float(np.pi)
SEG = 128


def phase_s5(kb, l, occ, col_major, MV, H, G, Y, s5par, s5bre, s5bim, s5cre, s5cim, s5vec, wglu, ident, Lx, Cx):
    T = Cx + Lx
    rows = Lx // GW
    with Phase(kb):
        idn = kb.sb("s_idn", [128, 128])
        vb = kb.sb("s_vb", [128, 24])
        kb.dma(idn.t[:], ident, writes=[idn])
        kb.dma(vb.t[:], s5vec[occ], writes=[vb])
        bufA = kb.sb("s_A", [128, T])
        bufB = kb.sb("s_B", [128, T])
        bufC = kb.sb("s_C", [128, T])
        names = ["lre", "lim", "dt", "mag", "ang", "cs", "sn", "cr", "ci", "nci", "t0", "t1", "t2"]
        cf = {n: kb.sb("s_cf_" + n, [128, 32]) for n in names}
        pt = kb.sb("s_pt", [128, 3, 32])
        ti32 = kb.sb("s_ti32", [128, 32], I32)
        Bre = kb.sb("s_Bre", [128, 4, 128])
        Bim = kb.sb("s_Bim", [128, 4, 128])
        Xre = kb.sb("s_Xre", [128, 4, 128])
        Xim = kb.sb("s_Xim", [128, 4, 128])
        WBre = kb.sb("s_WBre", [128, 4, 128])
        WBim = kb.sb("s_WBim", [128, 4, 128])
        Cre = kb.sb("s_Cre", [128, 4, 128])
        Cnim = kb.sb("s_Cnim", [128, 4, 128])
        Tc = kb.sb("s_Tc", [128, 4, 512])
        Ts = kb.sb("s_Ts", [128, 4, 512])
        Rt = kb.sb("s_Rt", [128, 4, 512])
        Ec = kb.sb("s_Ec", [128, 4])
        Es = kb.sb("s_Es", [128, 4])
        car = kb.sb("s_car", [128, 4, 2])
        ctmp = kb.sb("s_ctmp", [128, 2])
        ctmp2 = kb.sb("s_ctmp2", [128, 2])
        m1 = kb.sb("s_m1", [128, 512])
        m2 = kb.sb("s_m2", [128, 512])
        m3 = kb.sb("s_m3", [128, 512])
        m4 = kb.sb("s_m4", [128, 512])
        p1 = kb.sb("s_p1", [128, 512])
        p2 = kb.sb("s_p2", [128, 512])
        wres = [kb.sb("s_wre%d" % i, [128, 512]) for i in range(2)]
        wims = [kb.sb("s_wim%d" % i, [128, 512]) for i in range(2)]
        sre = [kb.sb("s_sre%d" % i, [128, 512]) for i in range(2)]
        sim = [kb.sb("s_sim%d" % i, [128, 512]) for i in range(2)]
        bur = [kb.ps("s_bur%d" % i) for i in range(2)]
        bui = [kb.ps("s_bui%d" % i) for i in range(2)]
        yps = [kb.ps("s_yps%d" % i) for i in range(2)]
        tps = kb.ps("s_tps")

        def c(n):
            return cf[n].t[:]

        def reduce_sin(dst, src_name, shift):
            ts(kb, "dve", c("t0"), c(src_name), shift, 1.0 / (2 * PI), ALU.add, ALU.mult, [cf[src_name]], [cf["t0"]])
            kb.op("dve", lambda e: e.tensor_copy(out=ti32.t[:], in_=c("t0")), [cf["t0"]], [ti32])
            kb.op("dve", lambda e: e.tensor_copy(out=c("t1"), in_=ti32.t[:]), [ti32], [cf["t1"]])
            ts(kb, "dve", c("t0"), c(src_name), shift, None, ALU.add, None, [cf[src_name]], [cf["t0"]])
            stt(kb, c("t0"), c("t1"), -2 * PI, c("t0"), ALU.mult, ALU.add, [cf["t1"], cf["t0"]], [cf["t0"]])
            ts(kb, "dve", c("t1"), c("t0"), PI, None, ALU.is_gt, None, [cf["t0"]], [cf["t1"]])
            stt(kb, c("t0"), c("t1"), -2 * PI, c("t0"), ALU.mult, ALU.add, [cf["t1"], cf["t0"]], [cf["t0"]])
            ts(kb, "dve", c("t1"), c("t0"), -PI, None, ALU.is_lt, None, [cf["t0"]], [cf["t1"]])
            stt(kb, c("t0"), c("t1"), 2 * PI, c("t0"), ALU.mult, ALU.add, [cf["t1"], cf["t0"]], [cf["t0"]])
            ts(kb, "dve", c("t0"), c("t0"), -PI, PI, ALU.max, ALU.min, [cf["t0"]], [cf["t0"]])
            act(kb, dst, c("t0"), AF.Sin, [cf["t0"]], [cf[dst_name[0]]])

        dst_name = [None]

        def coefs(d):
            kb.dma(pt.t[:], s5par[occ, d], writes=[pt])
            ts(kb, "dve", c("lre"), pt.t[:, 0, :], -1e-4, None, ALU.min, None, [pt], [cf["lre"]])
            kb.op("dve", lambda e: e.tensor_copy(out=c("lim"), in_=pt.t[:, 1, :]), [pt], [cf["lim"]])
            act(kb, c("dt"), pt.t[:, 2, :], AF.Exp, [pt], [cf["dt"]])
            tt(kb, c("t2"), c("lre"), c("dt"), ALU.mult, [cf["lre"], cf["dt"]], [cf["t2"]])
            act(kb, c("mag"), c("t2"), AF.Exp, [cf["t2"]], [cf["mag"]])
            tt(kb, c("ang"), c("lim"), c("dt"), ALU.mult, [cf["lim"], cf["dt"]], [cf["ang"]])
            dst_name[0] = "sn"
            reduce_sin(c("sn"), "ang", 0.0)
            dst_name[0] = "cs"
            reduce_sin(c("cs"), "ang", PI / 2)
            tt(kb, c("t0"), c("mag"), c("cs"), ALU.mult, [cf["mag"], cf["cs"]], [cf["t0"]])
            ts(kb, "dve", c("t0"), c("t0"), -1.0, None, ALU.add, None, [cf["t0"]], [cf["t0"]])
            tt(kb, c("t1"), c("mag"), c("sn"), ALU.mult, [cf["mag"], cf["sn"]], [cf["t1"]])
            tt(kb, c("t2"), c("lre"), c("lre"), ALU.mult, [cf["lre"]], [cf["t2"]])
            tt(kb, c("cr"), c("lim"), c("lim"), ALU.mult, [cf["lim"]], [cf["cr"]])
            tt(kb, c("t2"), c("t2"), c("cr"), ALU.add, [cf["t2"], cf["cr"]], [cf["t2"]])
            kb.op("dve", lambda e: e.reciprocal(out=c("t2"), in_=c("t2")), [cf["t2"]], [cf["t2"]])
            tt(kb, c("cr"), c("t0"), c("lre"), ALU.mult, [cf["t0"], cf["lre"]], [cf["cr"]])
            tt(kb, c("ci"), c("t1"), c("lim"), ALU.mult, [cf["t1"], cf["lim"]], [cf["ci"]])
            tt(kb, c("cr"), c("cr"), c("ci"), ALU.add, [cf["cr"], cf["ci"]], [cf["cr"]])
            tt(kb, c("cr"), c("cr"), c("t2"), ALU.mult, [cf["cr"], cf["t2"]], [cf["cr"]])
            tt(kb, c("ci"), c("t1"), c("lre"), ALU.mult, [cf["t1"], cf["lre"]], [cf["ci"]])
            tt(kb, c("nci"), c("t0"), c("lim"), ALU.mult, [cf["t0"], cf["lim"]], [cf["nci"]])
            tt(kb, c("ci"), c("ci"), c("nci"), ALU.subtract, [cf["ci"], cf["nci"]], [cf["ci"]])
            tt(kb, c("ci"), c("ci"), c("t2"), ALU.mult, [cf["ci"], cf["t2"]], [cf["ci"]])
            ts(kb, "dve", c("nci"), c("ci"), -1.0, None, ALU.mult, None, [cf["ci"]], [cf["nci"]])

        def col(n, q):
            return cf[n].t[:, q:q + 1]

        def unit_setup(d, k):
            kb.dma(Bre.t[:], s5bre[occ, d, :, 4 * k:4 * k + 4, :], writes=[Bre])
            kb.dma(Bim.t[:], s5bim[occ, d, :, 4 * k:4 * k + 4, :], writes=[Bim])
            kb.dma(Cre.t[:], s5cre[occ, d, :, 4 * k:4 * k + 4, :], writes=[Cre])
            kb.dma(Cnim.t[:], s5cim[occ, d, :, 4 * k:4 * k + 4, :], writes=[Cnim])
            ts(kb, "dve", Cnim.t[:], Cnim.t[:], -1.0, None, ALU.mult, None, [Cnim], [Cnim])
            for q4 in range(4):
                q = 4 * k + q4
                ts(kb, "dve", Xre.t[:, q4, :], Bre.t[:, q4, :], col("cr", q), None, ALU.mult, None, [Bre, cf["cr"]], [Xre])
                stt(kb, Xre.t[:, q4, :], Bim.t[:, q4, :], col("nci", q), Xre.t[:, q4, :], ALU.mult, ALU.add,
                    [Bim, cf["nci"], Xre], [Xre])
                ts(kb, "dve", Xim.t[:, q4, :], Bim.t[:, q4, :], col("cr", q), None, ALU.mult, None, [Bim, cf["cr"]], [Xim])
                stt(kb, Xim.t[:, q4, :], Bre.t[:, q4, :], col("ci", q), Xim.t[:, q4, :], ALU.mult, ALU.add,
                    [Bre, cf["ci"], Xim], [Xim])
                for X, W in ((Xre, WBre), (Xim, WBim)):
                    mm(kb, tps, X.t[:, q4, :], idn.t[:], True, True, [X, idn], ps_ap=tps.t[:, :128])
                    kb.op("act", lambda e: e.copy(out=W.t[:, q4, :], in_=tps.t[:, :128]), [tps], [W])
                kb.op("dve", lambda e: e.tensor_copy(out=Tc.t[:, q4, 0:1], in_=col("cs", q)), [cf["cs"]], [Tc])
                kb.op("dve", lambda e: e.tensor_copy(out=Ts.t[:, q4, 0:1], in_=col("sn", q)), [cf["sn"]], [Ts])
                m = 1
                while m < 512:
                    ec, es = Tc.t[:, q4, m - 1:m], Ts.t[:, q4, m - 1:m]
                    ts(kb, "dve", m1.t[:, :m], Ts.t[:, q4, 0:m], es, -1.0, ALU.mult, ALU.mult, [Ts], [m1])
                    stt(kb, Tc.t[:, q4, m:2 * m], Tc.t[:, q4, 0:m], ec, m1.t[:, :m], ALU.mult, ALU.add, [Tc, m1], [Tc])
                    ts(kb, "dve", m1.t[:, :m], Tc.t[:, q4, 0:m], es, None, ALU.mult, None, [Tc], [m1])
                    stt(kb, Ts.t[:, q4, m:2 * m], Ts.t[:, q4, 0:m], ec, m1.t[:, :m], ALU.mult, ALU.add, [Ts, m1], [Ts])
                    m *= 2
                kb.op("dve", lambda e: e.memset(Rt.t[:, q4, :], 1.0), writes=[Rt])
                ts(kb, "dve", Rt.t[:, q4, :], Rt.t[:, q4, :], col("mag", q), None, ALU.mult, None, [Rt, cf["mag"]], [Rt])
            kb.op("dve", lambda e: e.memset(car.t[:], 0.0), writes=[car])

        def unit_scan(uin, yout):
            items = []
            for ti, t0 in enumerate(range(0, T, 512)):
                for q4 in range(4):
                    items.append((ti, t0, min(512, T - t0), q4))

            def issue_bu(it):
                ti, t0, n, q4 = it
                br, bi = bur[q4 % 2], bui[q4 % 2]
                mm(kb, br, WBre.t[:, q4, :], uin.t[:, t0:t0 + n], True, True, [WBre, uin], ps_ap=br.t[:, :n])
                mm(kb, bi, WBim.t[:, q4, :], uin.t[:, t0:t0 + n], True, True, [WBim, uin], ps_ap=bi.t[:, :n])

            issue_bu(items[0])
            for ii, (ti, t0, n, q4) in enumerate(items):
                if ii + 1 < len(items):
                    issue_bu(items[ii + 1])
                yp = yps[ti % 2]
                br, bi = bur[q4 % 2], bui[q4 % 2]
                sr, si = sre[q4 % 2], sim[q4 % 2]
                wre, wim = wres[q4 % 2], wims[q4 % 2]
                tc, tsn = Tc.t[:, q4, :n], Ts.t[:, q4, :n]
                tt(kb, m1.t[:, :n], br.t[:, :n], tc, ALU.mult, [br, Tc], [m1])
                tt(kb, m2.t[:, :n], bi.t[:, :n], tsn, ALU.mult, [bi, Ts], [m2])
                tt(kb, m3.t[:, :n], bi.t[:, :n], tc, ALU.mult, [bi, Tc], [m3])
                tt(kb, m4.t[:, :n], br.t[:, :n], tsn, ALU.mult, [br, Ts], [m4])
                tt(kb, wre.t[:, :n], m1.t[:, :n], m2.t[:, :n], ALU.add, [m1, m2], [wre])
                tt(kb, wim.t[:, :n], m3.t[:, :n], m4.t[:, :n], ALU.subtract, [m3, m4], [wim])
                for w_, ci_ in ((wre, 0), (wim, 1)):
                    kb.op("dve", lambda e: e.tensor_tensor_scan(
                        out=w_.t[:, :n], data0=Rt.t[:, q4, :n], data1=w_.t[:, :n],
                        initial=car.t[:, q4, ci_:ci_ + 1], op0=ALU.mult, op1=ALU.add), [Rt, w_, car], [w_])
                we_r, we_i = wre.t[:, n - 1:n], wim.t[:, n - 1:n]
                ec, es = Tc.t[:, q4, n - 1:n], Ts.t[:, q4, n - 1:n]
                ts(kb, "dve", ctmp2.t[:, 0:1], we_r, es, None, ALU.mult, None, [wre, Ts], [ctmp2])
                ts(kb, "dve", ctmp.t[:, 0:1], we_i, es, -1.0, ALU.mult, ALU.mult, [wim, Ts], [ctmp])
                stt(kb, car.t[:, q4, 1:2], we_i, ec, ctmp2.t[:, 0:1], ALU.mult, ALU.add, [wim, Tc, ctmp2], [car])
                stt(kb, car.t[:, q4, 0:1], we_r, ec, ctmp.t[:, 0:1], ALU.mult, ALU.add, [wre, Tc, ctmp], [car])
                tt(kb, p1.t[:, :n], wre.t[:, :n], tc, ALU.mult, [wre, Tc], [p1], eng="pool")
                tt(kb, p2.t[:, :n], wim.t[:, :n], tsn, ALU.mult, [wim, Ts], [p2], eng="pool")
                tt(kb, sr.t[:, :n], p1.t[:, :n], p2.t[:, :n], ALU.subtract, [p1, p2], [sr], eng="pool")
                tt(kb, p1.t[:, :n], wre.t[:, :n], tsn, ALU.mult, [wre, Ts], [p1], eng="pool")
                tt(kb, p2.t[:, :n], wim.t[:, :n], tc, ALU.mult, [wim, Tc], [p2], eng="pool")
                tt(kb, si.t[:, :n], p1.t[:, :n], p2.t[:, :n], ALU.add, [p1, p2], [si], eng="pool")
                mm(kb, yp, Cre.t[:, q4, :], sr.t[:, :n], q4 == 0, False, [Cre, sr], ps_ap=yp.t[:, :n])
                mm(kb, yp, Cnim.t[:, q4, :], si.t[:, :n], False, q4 == 3, [Cnim, si], ps_ap=yp.t[:, :n])
                if q4 == 3:
                    kb.op("act", lambda e: e.copy(out=yout.t[:, t0:t0 + n], in_=yp.t[:, :n]), [yp], [yout])

        def lat_view(buf, cm):
            v = buf.t[:, Cx:T]
            if cm:
                return v.rearrange("p (r c) -> p c r", c=GW)
            return v

        for d in range(2):
            coefs(d)
            for k in range(8):
                unit_setup(d, k)
                kb.dma(bufA.t[:], H.t[:, k, :], writes=[bufA])
                ts(kb, "dve", bufB.t[:, 0:Cx], bufA.t[:, 0:Cx], MV.t[:, l, 8 + k, 1:2], MV.t[:, l, k, 1:2],
                   ALU.mult, ALU.add, [bufA, MV], [bufB])
                if col_major:
                    ts(kb, "dve", bufB.t[:, Cx:T].rearrange("p (c r) -> p c r", r=rows), lat_view(bufA, True),
                       MV.t[:, l, 8 + k, 0:1], MV.t[:, l, k, 0:1], ALU.mult, ALU.add, [bufA, MV], [bufB])
                else:
                    ts(kb, "dve", bufB.t[:, Cx:T], bufA.t[:, Cx:T], MV.t[:, l, 8 + k, 0:1], MV.t[:, l, k, 0:1],
                       ALU.mult, ALU.add, [bufA, MV], [bufB])
                if d == 0:
                    unit_scan(bufB, bufA)
                    stt(kb, bufA.t[:], bufB.t[:], vb.t[:, k:k + 1], bufA.t[:], ALU.mult, ALU.add, [bufB, vb, bufA], [bufA])
                    kb.dma(G.t[:, k, :], bufA.t[:], reads=[bufA])
                else:
                    kb.op("dve", lambda e: e.tensor_copy(out=bufC.t[:, 0:Cx], in_=bufB.t[:, 0:Cx][:, ::-1]), [bufB], [bufC])
                    kb.op("dve", lambda e: e.tensor_copy(out=bufC.t[:, Cx:T], in_=bufB.t[:, Cx:T][:, ::-1]), [bufB], [bufC])
                    unit_scan(bufC, bufA)
                    kb.dma(bufB.t[:], G.t[:, k, :], writes=[bufB])
                    tt(kb, bufB.t[:, 0:Cx], bufB.t[:, 0:Cx], bufA.t[:, 0:Cx][:, ::-1], ALU.add, [bufB, bufA], [bufB])
                    tt(kb, bufB.t[:, Cx:T], bufB.t[:, Cx:T], bufA.t[:, Cx:T][:, ::-1], ALU.add, [bufB, bufA], [bufB])
                    act(kb, bufC.t[:, 0:Cx], bufB.t[:, 0:Cx], AF.Gelu, [bufB], [bufC])
                    if col_major:
                        act(kb, lat_view(bufC, True), bufB.t[:, Cx:T].rearrange("p (c r) -> p c r", r=rows), AF.Gelu,
                            [bufB], [bufC])
                    else:
                        act(kb, bufC.t[:, Cx:T], bufB.t[:, Cx:T], AF.Gelu, [bufB], [bufC])
                    kb.dma(G.t[:, k, :], bufC.t[:], reads=[bufC])
            barrier(kb)
    with Phase(kb):
        vb = kb.sb("g_vb", [128, 24])
        wl = kb.sb("g_w", [128, 8, 2048], BF16)
        gb = kb.sb("g_gb", [128, 8, 512], BF16)
        sg = kb.sb("g_sg", [128, 512])
        ob = kb.sb("g_ob", [128, 8, 512])
        Ap = [kb.ps("g_Ap%d" % i) for i in range(2)]
        Bp = [kb.ps("g_Bp%d" % i) for i in range(2)]
        kb.dma(vb.t[:], s5vec[occ], writes=[vb])
        for k in range(8):
            kb.dma(wl.t[:, k, :], wglu[occ, :, k, :], writes=[wl], q="pool")
        for t0 in range(0, T, 512):
            n = min(512, T - t0)
            kb.dma(gb.t[:, :, :n], G.t[:, :, t0:t0 + n], writes=[gb], q="pool")
            for j in range(8):
                A, Bq = Ap[j % 2], Bp[j % 2]
                for k in range(8):
                    mm(kb, A, wl.t[:, k, j * 128:(j + 1) * 128], gb.t[:, k, :n], k == 0, k == 7, [wl, gb], ps_ap=A.t[:, :n])
                for k in range(8):
                    mm(kb, Bq, wl.t[:, k, 1024 + j * 128:1024 + (j + 1) * 128], gb.t[:, k, :n], k == 0, k == 7,
                       [wl, gb], ps_ap=Bq.t[:, :n])
                act(kb, sg.t[:, :n], Bq.t[:, :n], AF.Sigmoid, [Bq, vb], [sg], bias=vb.t[:, 16 + j:17 + j])
                stt(kb, ob.t[:, j, :n], A.t[:, :n], vb.t[:, 8 + j:9 + j], sg.t[:, :n], ALU.add, ALU.mult, [A, vb, sg], [ob])
            kb.dma(Y.t[:, :, t0:t0 + n], ob.t[:, :, :n], reads=[ob])


def host_s5(inp):
    m = {}
    NA = inp["s5_lambda_re"].shape[0]
    def par(a):
        return a.reshape(NA, 2, 32, 2, 64).transpose(0, 1, 3, 4, 2).reshape(NA, 2, 128, 32)
    ldt = np.broadcast_to(inp["s5_log_dt"][..., None], inp["s5_lambda_re"].shape)
    m["s5par"] = np.ascontiguousarray(np.stack([par(inp["s5_lambda_re"]), par(inp["s5_lambda_im"]), par(ldt)], axis=3))
    def masked(a_gpc):
        out = np.zeros((NA, 2, 2, 64, 32, 8, 16), np.float32)
        for g in range(64):
            q, gl = g // 2, g % 2
            out[:, :, gl, :, q, g % 8, :] = a_gpc[:, :, g]
        return out.reshape(NA, 2, 128, 32, 128)
    m["s5bre"] = masked(inp["s5_b_re"])
    m["s5bim"] = masked(inp["s5_b_im"])
    m["s5cre"] = masked(inp["s5_c_re"].transpose(0, 1, 2, 4, 3))
    m["s5cim"] = masked(inp["s5_c_im"].transpose(0, 1, 2, 4, 3))
    m["s5vec"] = np.ascontiguousarray(np.stack(
        [np.concatenate([fm(inp["s5_d"][o]), fm(inp["s5_b_glu"][o])], axis=1) for o in range(NA)], 0))
    m["wglu"] = np.ascontiguousarray(inp["s5_w_glu"].reshape(NA, 8, 128, 2048).transpose(0, 2, 1, 3))
    m["ident"] = np.eye(128, dtype=np.float32)
    return m


E2 = 2048
NE = 16
NV_ML = 48 + 16 + 16 + 16
CH = 128


def phase_mlstm(kb, l, MV, H, Y, XM, XC, Z, QT, KT, VT, GT, HH, ml_wup, ml_wq, ml_wk, ml_wv, ml_wg, ml_bg, mlvec,
                ml_wdown, ident, tri, Lx, Cx):
    T = Cx + Lx
    with Phase(kb):
        vb = kb.sb("a_vb", [128, NV_ML])
        kb.dma(vb.t[:], mlvec, writes=[vb])
        NW = 258
        hb = kb.sb("a_hb", [128, 8, NW])
        ub = kb.sb("a_ub", [128, 8, NW], BF16)
        xmp = kb.sb("a_xmp", [128, NE, NW])
        zb = kb.sb("a_zb", [128, NE, 256])
        xcb = kb.sb("a_xcb", [128, NE, 256])
        acc = kb.sb("a_acc", [128, 256])
        wup = kb.sb("a_wup", [128, 32, 8, 128], BF16)
        for j in range(32):
            kb.dma(wup.t[:, j], ml_wup[j], writes=[wup], q="pool")
        pp = [kb.ps("a_ps%d" % i) for i in range(2)]
        cnt = [0]

        def tile(seq0, seqn, p0, n2, s):
            lo = max(p0 - 1, 0)
            hi = min(p0 + n2 + 1, seqn)
            n1 = hi - lo
            off = lo - (p0 - 1)
            io = p0 - lo
            kb.dma(hb.t[:, :, :n1], H.t[:, :, seq0 + lo:seq0 + hi], writes=[hb])
            if n1 < n2 + 2:
                kb.op("dve", lambda e: e.memset(xmp.t[:], 0.0), writes=[xmp])
            for k in range(8):
                ts(kb, "dve", ub.t[:, k, :n1], hb.t[:, k, :n1], MV.t[:, l, 8 + k, s:s + 1], MV.t[:, l, k, s:s + 1],
                   ALU.mult, ALU.add, [hb, MV], [ub])
            for j in range(32):
                ps = pp[cnt[0] % 2]
                cnt[0] += 1
                for k in range(8):
                    mm(kb, ps, wup.t[:, j, k, :], ub.t[:, k, :n1], k == 0, k == 7, [wup, ub], ps_ap=ps.t[:, :n1])
                if j < NE:
                    kb.op("act", lambda e: e.copy(out=xmp.t[:, j, off:off + n1], in_=ps.t[:, :n1]), [ps], [xmp])
                else:
                    kb.op("act", lambda e: e.copy(out=zb.t[:, j - NE, :n2], in_=ps.t[:, io:io + n2]), [ps], [zb])
            for j in range(NE):
                ts(kb, "dve", acc.t[:, :n2], xmp.t[:, j, 0:n2], vb.t[:, j:j + 1], vb.t[:, 48 + j:49 + j], ALU.mult, ALU.add,
                   [xmp, vb], [acc])
                stt(kb, acc.t[:, :n2], xmp.t[:, j, 1:1 + n2], vb.t[:, 16 + j:17 + j], acc.t[:, :n2], ALU.mult, ALU.add,
                    [xmp, vb, acc], [acc])
                stt(kb, acc.t[:, :n2], xmp.t[:, j, 2:2 + n2], vb.t[:, 32 + j:33 + j], acc.t[:, :n2], ALU.mult, ALU.add,
                    [xmp, vb, acc], [acc])
                act(kb, xcb.t[:, j, :n2], acc.t[:, :n2], AF.Silu, [acc], [xcb])
            a, b_ = seq0 + p0, seq0 + p0 + n2
            kb.dma(XM.t[:, :, a:b_], xmp.t[:, :, 1:1 + n2], reads=[xmp])
            kb.dma(Z.t[:, :, a:b_], zb.t[:, :, :n2], reads=[zb])
            kb.dma(XC.t[:, :, a:b_], xcb.t[:, :, :n2], reads=[xcb])

        for p0 in range(0, Cx, 256):
            tile(0, Cx, p0, min(256, Cx - p0), 1)
        for p0 in range(0, Lx, 256):
            tile(Cx, Lx, p0, 256, 0)
    with Phase(kb):
        xcb = kb.sb("b_xcb", [128, NE, 512], BF16)
        xmb = kb.sb("b_xmb", [128, NE, 512], BF16)
        qkv = [kb.sb("b_qkv%d" % i, [128, NE, 512]) for i in range(3)]
        wch = [kb.sb("b_w%d" % i, [128, NE, 128], BF16) for i in range(2)]
        wg = kb.sb("b_wg", [128, 3, NE, 16])
        bg = kb.sb("b_bg", [16, 1])
        gtb = kb.sb("b_gt", [16, 512])
        pp = [kb.ps("b_ps%d" % i) for i in range(2)]
        gp = kb.ps("b_gp")
        kb.dma(wg.t[:], ml_wg, writes=[wg])
        kb.dma(bg.t[:], ml_bg, writes=[bg])
        cnt = [0]
        for t0 in range(0, T, 512):
            n = min(512, T - t0)
            kb.dma(xcb.t[:, :, :n], XC.t[:, :, t0:t0 + n], writes=[xcb], q="pool")
            kb.dma(xmb.t[:, :, :n], XM.t[:, :, t0:t0 + n], writes=[xmb], q="pool")
            for i, (W, src, dst, scale) in enumerate(((ml_wq, xcb, QT, 1.0), (ml_wk, xcb, KT, 512 ** -0.5), (ml_wv, xmb, VT, 1.0))):
                ob = qkv[i]
                for j in range(NE):
                    w = wch[cnt[0] % 2]
                    ps = pp[cnt[0] % 2]
                    cnt[0] += 1
                    kb.dma(w.t[:], W[j], writes=[w], q="pool")
                    for k in range(NE):
                        mm(kb, ps, w.t[:, k, :], src.t[:, k, :n], k == 0, k == NE - 1, [w, src], ps_ap=ps.t[:, :n])
                    kb.op("act", lambda e: e.mul(ob.t[:, j, :n], ps.t[:, :n], scale), [ps], [ob])
                kb.dma(dst.t[:, :, t0:t0 + n], ob.t[:, :, :n], reads=[ob])
            first = True
            for i in range(3):
                for k in range(NE):
                    mm(kb, gp, wg.t[:, i, k, :], qkv[i].t[:, k, :n], first, (i == 2 and k == NE - 1), [wg, qkv[i]],
                       ps_ap=gp.t[0:16, :n])
                    first = False
            ts(kb, "dve", gtb.t[:, :n], gp.t[0:16, :n], bg.t[:, 0:1], None, ALU.add, None, [gp, bg], [gtb])
            kb.dma(GT.t[:, t0:t0 + n], gtb.t[:, :n], reads=[gtb])
    with Phase(kb):
        idn = kb.sb("m_idn", [128, 128])
        trf = kb.sb("m_trf", [128, 128])
        trb = kb.sb("m_trb", [128, 128])
        ngf = kb.sb("m_ngf", [128, 128])
        ngb = kb.sb("m_ngb", [128, 128])
        ones = kb.sb("m_ones", [128, 128])
        kb.dma(idn.t[:], ident, writes=[idn])
        kb.dma(trf.t[:], tri, writes=[trf])
        kb.dma(trb.t[:], tri.rearrange("a b -> b a"), writes=[trb]) if False else None
        kb.op("dve", lambda e: e.memset(ones.t[:], 1.0), writes=[ones])
        tp = kb.ps("m_tp")
        mm(kb, tp, trf.t[:], idn.t[:], True, True, [trf, idn], ps_ap=tp.t[:, :128])
        kb.op("act", lambda e: e.copy(out=trb.t[:], in_=tp.t[:, :128]), [tp], [trb])
        for tr, ng in ((trf, ngf), (trb, ngb)):
            ts(kb, "dve", ng.t[:], tr.t[:], -1.0, 30000.0, ALU.add, ALU.mult, [tr], [ng])
        qcs = [kb.sb("m_qc%d" % i, [128, 4, CH], BF16) for i in range(2)]
        kcs = [kb.sb("m_kc%d" % i, [128, 4, CH], BF16) for i in range(2)]
        vcs = [kb.sb("m_vc%d" % i, [128, 4, CH], BF16) for i in range(2)]
        gcs = [kb.sb("m_gc%d" % i, [16, CH]) for i in range(2)]
        holds = [kb.sb("m_hold%d" % i, [128, 4, CH]) for i in range(2)]
        idb = kb.sb("m_idb", [128, 128], BF16)
        oneb = kb.sb("m_oneb", [128, 128], BF16)
        kb.op("dve", lambda e: e.tensor_copy(out=idb.t[:], in_=idn.t[:]), [idn], [idb])
        kb.op("dve", lambda e: e.memset(oneb.t[:], 1.0), writes=[oneb])
        qs = kb.sb("m_qs", [128, 4, CH], BF16)
        vtm = kb.sb("m_vtm", [128, 512], BF16)
        ktm = kb.sb("m_ktm", [128, 512], BF16)
        wv = kb.sb("m_wv", [128, 512], BF16)
        cols = kb.sb("m_cols", [128, 16])
        wcb = kb.sb("m_wcb", [128, 2], BF16)
        Lt = kb.sb("m_Lt", [128, 128])
        Em = kb.sb("m_Em", [128, 128])
        DT = kb.sb("m_DT", [128, 128])
        ebm = kb.sb("m_ebm", [128, 128])
        ST = kb.sb("m_ST", [128, 128], BF16)
        rden = kb.sb("m_rden", [128, 128])
        hc = kb.sb("m_hc", [128, 4, CH])
        Cst = kb.sb("m_Cst", [128, 4, 512])
        Csb = kb.sb("m_Csb", [128, 4, 512], BF16)
        nst = kb.sb("m_nst", [128, 4])
        nrep = kb.sb("m_nrep", [128, 4, 128], BF16)
        P0 = kb.ps("m_P0")
        P1 = kb.ps("m_P1")
        P2 = kb.ps("m_P2")
        P3 = kb.ps("m_P3")
        P4 = kb.ps("m_P4")
        P5 = kb.ps("m_P5")
        P6 = kb.ps("m_P6")

        def loads(d, h, c0, sl):
            kb.dma(qcs[sl].t[:], QT.t[:, 4 * h:4 * h + 4, c0:c0 + CH], writes=[qcs[sl]], q="pool")
            kb.dma(kcs[sl].t[:], KT.t[:, 4 * h:4 * h + 4, c0:c0 + CH], writes=[kcs[sl]], q="pool")
            kb.dma(vcs[sl].t[:], VT.t[:, 4 * h:4 * h + 4, c0:c0 + CH], writes=[vcs[sl]], q="pool")
            kb.dma(gcs[sl].t[:], GT.t[:, c0:c0 + CH], writes=[gcs[sl]])
            if d == 1:
                kb.dma(holds[sl].t[:], HH.t[:, 4 * h:4 * h + 4, c0:c0 + CH], writes=[holds[sl]])

        def chunk(d, h, c0, sl):
            tr, ng = (trf, ngf) if d == 0 else (trb, ngb)
            last = CH - 1 if d == 0 else 0
            qc, kc, vc, gc, hold = qcs[sl], kcs[sl], vcs[sl], gcs[sl], holds[sl]
            mm(kb, P0, gc.t[:, :], idn.t[0:16, 0:16], True, True, [gc, idn], ps_ap=P0.t[:, 0:16])
            ic, fc = d * 8 + h, d * 8 + 4 + h
            act(kb, cols.t[:, 8:9], P0.t[:, fc:fc + 1], AF.Exp, [P0], [cols], scale=-1.0)
            act(kb, cols.t[:, 0:1], cols.t[:, 8:9], AF.Ln, [cols], [cols], bias=1.0)
            kb.op("dve", lambda e: e.tensor_copy(out=cols.t[:, 1:2], in_=cols.t[:, 0:1]), [cols], [cols])
            kb.op("dve", lambda e: e.tensor_copy(out=cols.t[:, 2:3], in_=P0.t[:, ic:ic + 1]), [P0], [cols])
            mm(kb, P0, tr.t[:], cols.t[:, 0:2], True, True, [tr, cols], ps_ap=P0.t[:, 16:18])
            ts(kb, "dve", Lt.t[:], ones.t[:], cols.t[:, 0:1], None, ALU.mult, None, [ones, cols], [Lt])
            mm(kb, P1, Lt.t[:], tr.t[:], True, True, [Lt, tr], ps_ap=P1.t[:, :128])
            tt(kb, cols.t[:, 3:4], cols.t[:, 2:3], P0.t[:, 16:17], ALU.add, [cols, P0], [cols])
            ts(kb, "dve", Em.t[:], P1.t[:, :128], -1.0, cols.t[:, 3:4], ALU.mult, ALU.add, [P1, cols], [Em])
            tt(kb, Em.t[:], Em.t[:], tr.t[:], ALU.mult, [Em, tr], [Em])
            tt(kb, Em.t[:], Em.t[:], ng.t[:], ALU.add, [Em, ng], [Em])
            act(kb, DT.t[:], Em.t[:], AF.Exp, [Em], [DT])
            act(kb, ebm.t[:], P1.t[:, :128], AF.Exp, [P1], [ebm], scale=-1.0)
            ts(kb, "dve", cols.t[:, 4:5], P1.t[:, last:last + 1], -1.0, None, ALU.mult, None, [P1], [cols])
            act(kb, cols.t[:, 5:6], cols.t[:, 4:5], AF.Exp, [cols], [cols])
            act(kb, cols.t[:, 6:7], cols.t[:, 3:4], AF.Exp, [cols], [cols], bias=cols.t[:, 4:5])
            kb.op("dve", lambda e: e.tensor_copy(out=wcb.t[:, 0:1], in_=cols.t[:, 6:7]), [cols], [wcb])
            kb.op("dve", lambda e: e.tensor_copy(out=wcb.t[:, 1:2], in_=cols.t[:, 6:7]), [cols], [wcb])
            for dt_ in range(4):
                mm(kb, P2, kc.t[:, dt_, :], qc.t[:, dt_, :], dt_ == 0, dt_ == 3, [kc, qc], ps_ap=P2.t[:, :128])
            tt(kb, ST.t[:], DT.t[:], P2.t[:, :128], ALU.mult, [DT, P2], [ST])
            for dt_ in range(4):
                tt(kb, qs.t[:, dt_, :], qc.t[:, dt_, :], ebm.t[:], ALU.mult, [qc, ebm], [qs])
            for et in range(4):
                mm(kb, P3, vc.t[:, et, :], idb.t[:], True, True, [vc, idb], ps_ap=P3.t[:, et * 128:(et + 1) * 128])
            kb.op("act", lambda e: e.copy(out=vtm.t[:], in_=P3.t[:]), [P3], [vtm])
            for et in range(4):
                mm(kb, P4, kc.t[:, et, :], idb.t[:], True, True, [kc, idb], ps_ap=P4.t[:, et * 128:(et + 1) * 128])
            kb.op("act", lambda e: e.copy(out=ktm.t[:], in_=P4.t[:]), [P4], [ktm])
            mm(kb, P6, oneb.t[:], ST.t[:], True, False, [oneb, ST], ps_ap=P6.t[:, :128])
            for dt_ in range(4):
                mm(kb, P6, nrep.t[:, dt_, :], qs.t[:, dt_, :], False, dt_ == 3, [nrep, qs], ps_ap=P6.t[:, :128])
            ts(kb, "dve", rden.t[:], P6.t[:, :128], -1.0, None, ALU.mult, None, [P6], [rden])
            tt(kb, rden.t[:], rden.t[:], P6.t[:, :128], ALU.max, [rden, P6], [rden])
            ts(kb, "dve", rden.t[:], rden.t[:], 1.0, None, ALU.max, None, [rden], [rden])
            kb.op("dve", lambda e: e.reciprocal(out=rden.t[:], in_=rden.t[:]), [rden], [rden])
            for et in range(4):
                o = P5.t[:, et * 128:(et + 1) * 128]
                mm(kb, P5, vtm.t[:, et * 128:(et + 1) * 128], ST.t[:], True, False, [vtm, ST], ps_ap=o)
                for dt_ in range(4):
                    mm(kb, P5, Csb.t[:, dt_, et * 128:(et + 1) * 128], qs.t[:, dt_, :], False, dt_ == 3, [Csb, qs], ps_ap=o)
                tt(kb, hc.t[:, et, :], rden.t[:], o, ALU.mult, [rden, P5], [hc])
                if d == 1:
                    tt(kb, hc.t[:, et, :], hc.t[:, et, :], hold.t[:, et, :], ALU.add, [hc, hold], [hc])
            kb.dma(HH.t[:, 4 * h:4 * h + 4, c0:c0 + CH], hc.t[:], reads=[hc])
            ts(kb, "dve", wv.t[:], vtm.t[:], cols.t[:, 6:7], None, ALU.mult, None, [vtm, cols], [wv])
            for dt_ in range(4):
                mm(kb, P3, ktm.t[:, dt_ * 128:(dt_ + 1) * 128], wv.t[:], True, True, [ktm, wv], ps_ap=P3.t[:])
                stt(kb, Cst.t[:, dt_, :], Cst.t[:, dt_, :], cols.t[:, 5:6], P3.t[:], ALU.mult, ALU.add, [Cst, cols, P3], [Cst])
                kb.op("act", lambda e: e.copy(out=Csb.t[:, dt_, :], in_=Cst.t[:, dt_, :]), [Cst], [Csb])
                mm(kb, P4, ktm.t[:, dt_ * 128:(dt_ + 1) * 128], wcb.t[:, 0:2], True, True, [ktm, wcb], ps_ap=P4.t[:, 0:2])
                stt(kb, nst.t[:, dt_:dt_ + 1], nst.t[:, dt_:dt_ + 1], cols.t[:, 5:6], P4.t[:, 0:1], ALU.mult, ALU.add,
                    [nst, cols, P4], [nst])
                ts(kb, "dve", nrep.t[:, dt_, :], ones.t[:], nst.t[:, dt_:dt_ + 1], None, ALU.mult, None, [ones, nst], [nrep])

        nc_ctx, nc_all = Cx // CH, T // CH
        for d in range(2):
            units = []
            for h in range(4):
                if d == 0:
                    order = list(range(nc_all))
                else:
                    order = list(range(nc_ctx - 1, -1, -1)) + list(range(nc_all - 1, nc_ctx - 1, -1))
                units += [(h, ci, i == 0) for i, ci in enumerate(order)]
            loads(d, units[0][0], units[0][1] * CH, 0)
            for ui, (h, ci, first) in enumerate(units):
                if ui + 1 < len(units):
                    loads(d, units[ui + 1][0], units[ui + 1][1] * CH, (ui + 1) % 2)
                if first:
                    kb.op("dve", lambda e: e.memset(Cst.t[:], 0.0), writes=[Cst])
                    kb.op("dve", lambda e: e.memset(Csb.t[:], 0.0), writes=[Csb])
                    kb.op("dve", lambda e: e.memset(nst.t[:], 0.0), writes=[nst])
                    kb.op("dve", lambda e: e.memset(nrep.t[:], 0.0), writes=[nrep])
                chunk(d, h, ci * CH, ui % 2)
            barrier(kb)
    with Phase(kb):
        vb = kb.sb("o_vb", [128, NV_ML])
        kb.dma(vb.t[:], mlvec, writes=[vb])
        wd = kb.sb("o_wd", [128, NE, 1024], BF16)
        hmb = kb.sb("o_hmb", [128, NE, 256], BF16)
        for k in range(NE):
            kb.dma(wd.t[:, k, :], ml_wdown[:, k, :], writes=[wd], q="pool")
        hb = kb.sb("o_hb", [128, NE, 256])
        xcb = kb.sb("o_xcb", [128, NE, 256])
        zb = kb.sb("o_zb", [128, NE, 256])
        ob = kb.sb("o_ob", [128, 8, 256])
        ones = kb.sb("o_ones", [128, 128])
        sq = kb.sb("o_sq", [128, 256])
        mean = kb.sb("o_mean", [128, 256])
        m2 = kb.sb("o_m2", [128, 256])
        rstd = kb.sb("o_rstd", [128, 256])
        tmp = kb.sb("o_tmp", [128, 256])
        tmp2 = kb.sb("o_tmp2", [128, 256])
        S1 = kb.ps("o_S1")
        S2 = kb.ps("o_S2")
        pp = [kb.ps("o_ps%d" % i) for i in range(2)]
        kb.op("dve", lambda e: e.memset(ones.t[:], 1.0), writes=[ones])
        for t0 in range(0, T, 256):
            n = min(256, T - t0)
            kb.dma(hb.t[:, :, :n], HH.t[:, :, t0:t0 + n], writes=[hb])
            kb.dma(xcb.t[:, :, :n], XC.t[:, :, t0:t0 + n], writes=[xcb])
            kb.dma(zb.t[:, :, :n], Z.t[:, :, t0:t0 + n], writes=[zb])
            for h in range(4):
                for et in range(4):
                    j = 4 * h + et
                    act(kb, sq.t[:, :n], hb.t[:, j, :n], AF.Square, [hb], [sq])
                    mm(kb, S1, ones.t[:], hb.t[:, j, :n], et == 0, et == 3, [ones, hb], ps_ap=S1.t[:, :n])
                    mm(kb, S2, ones.t[:], sq.t[:, :n], et == 0, et == 3, [ones, sq], ps_ap=S2.t[:, :n])
                kb.op("act", lambda e: e.mul(mean.t[:, :n], S1.t[:, :n], 1.0 / 512), [S1], [mean])
                tt(kb, m2.t[:, :n], mean.t[:, :n], mean.t[:, :n], ALU.mult, [mean], [m2])
                stt(kb, m2.t[:, :n], S2.t[:, :n], 1.0 / 512, m2.t[:, :n], ALU.mult, ALU.subtract, [S2, m2], [m2])
                ts(kb, "dve", m2.t[:, :n], m2.t[:, :n], EPS, None, ALU.add, None, [m2], [m2])
                act(kb, m2.t[:, :n], m2.t[:, :n], AF.Sqrt, [m2], [m2])
                kb.op("dve", lambda e: e.reciprocal(out=rstd.t[:, :n], in_=m2.t[:, :n]), [m2], [rstd])
                for et in range(4):
                    j = 4 * h + et
                    tt(kb, tmp.t[:, :n], hb.t[:, j, :n], mean.t[:, :n], ALU.subtract, [hb, mean], [tmp])
                    tt(kb, tmp.t[:, :n], tmp.t[:, :n], rstd.t[:, :n], ALU.mult, [tmp, rstd], [tmp])
                    ts(kb, "dve", tmp.t[:, :n], tmp.t[:, :n], vb.t[:, 64 + j:65 + j], None, ALU.mult, None, [tmp, vb], [tmp])
                    stt(kb, tmp.t[:, :n], xcb.t[:, j, :n], vb.t[:, 80 + j:81 + j], tmp.t[:, :n], ALU.mult, ALU.add,
                        [xcb, vb, tmp], [tmp])
                    act(kb, tmp2.t[:, :n], zb.t[:, j, :n], AF.Silu, [zb], [tmp2])
                    tt(kb, hmb.t[:, j, :n], tmp.t[:, :n], tmp2.t[:, :n], ALU.mult, [tmp, tmp2], [hmb])
            for j in range(8):
                ps = pp[j % 2]
                for k in range(NE):
                    mm(kb, ps, wd.t[:, k, j * 128:(j + 1) * 128], hmb.t[:, k, :n], k == 0, k == NE - 1, [wd, hmb], ps_ap=ps.t[:, :n])
                kb.op("act", lambda e: e.copy(out=ob.t[:, j, :n], in_=ps.t[:, :n]), [ps], [ob])
            kb.dma(Y.t[:, :, t0:t0 + n], ob.t[:, :, :n], reads=[ob])


def host_ml(inp):
    m = {}
    m["ml_wup"] = np.ascontiguousarray(inp["ml_w_up"][0].reshape(8, 128, 32, 128).transpose(2, 1, 0, 3))
    for nm in ("q", "k", "v"):
        m["ml_w" + nm] = np.ascontiguousarray(inp["ml_w_" + nm][0].reshape(16, 128, 16, 128).transpose(2, 1, 0, 3))
    m["ml_wg"] = np.ascontiguousarray(inp["ml_w_gates"][0].reshape(3, 16, 128, 16).transpose(2, 0, 1, 3))
    m["ml_bg"] = np.ascontiguousarray(inp["ml_b_gates"][0].reshape(16, 1))
    cw = inp["ml_conv_w"][0]
    cols = [fm(cw[0]), fm(cw[1]), fm(cw[2]), fm(inp["ml_conv_b"][0]), fm(inp["ml_gn_g"][0]), fm(inp["ml_skip"][0])]
    m["mlvec"] = np.ascontiguousarray(np.concatenate(cols, axis=1).astype(np.float32))
    m["ml_wdown"] = np.ascontiguousarray(inp["ml_w_down"][0].reshape(16, 128, 1024).transpose(1, 0, 2))
    m["ident"] = np.eye(128, dtype=np.float32)
    m["tri"] = np.triu(np.ones((128, 128), np.float32))
    return m


def build_mega(Lx, Cx, nlayers=DEPTH):
    kb = KB()
    T = Cx + Lx
    cT = kb.din("cT", [128, 8, 2])
    modw = kb.din("modw", [DEPTH, 128, 8, 6144])
    modb = kb.din("modb", [128, DEPTH, 48])
    H0 = kb.dram("hT", [128, 8, T], kind="ExternalInput")
    OUT = kb.dram("oT", [128, 8, T], kind="ExternalOutput")
    p2vec = kb.din("p2vec", [DEPTH, 128, NV_P2])
    wg = kb.din("wg", [DEPTH, NF, 128, 8, 128])
    wu = kb.din("wu", [DEPTH, NF, 128, 8, 128])
    wd = kb.din("wd", [DEPTH, 8, 128, NF, 128])
    s5in = (kb.din("s5par", [2, 2, 128, 3, 32]), kb.din("s5bre", [2, 2, 128, 32, 128]),
            kb.din("s5bim", [2, 2, 128, 32, 128]), kb.din("s5cre", [2, 2, 128, 32, 128]),
            kb.din("s5cim", [2, 2, 128, 32, 128]), kb.din("s5vec", [2, 128, 24]),
            kb.din("wglu", [2, 128, 8, 2048]))
    ident = kb.din("ident", [128, 128])
    tri = kb.din("tri", [128, 128])
    mlin = (kb.din("ml_wup", [32, 128, 8, 128]), kb.din("ml_wq", [16, 128, 16, 128]),
            kb.din("ml_wk", [16, 128, 16, 128]), kb.din("ml_wv", [16, 128, 16, 128]),
            kb.din("ml_wg", [128, 3, 16, 16]), kb.din("ml_bg", [16, 1]), kb.din("mlvec", [128, NV_ML]),
            kb.din("ml_wdown", [128, 16, 1024]))
    cvin = (kb.din("cvvec", [128, NV_CV]), kb.din("cv_win", [128, 8, 2048]), kb.din("cv_wout", [128, 8, 1024]))
    HA = kb.dram("HA", [128, 8, T])
    HB = kb.dram("HB", [128, 8, T])
    Yb = kb.dram("Yb", [128, 8, T])
    Gs = kb.dram("Gs", [128, 8, T])
    big = [kb.dram(nm, [128, 16, T]) for nm in ("XM", "XC", "Z", "QT", "KT", "VT", "HH")]
    GT = kb.dram("GT", [16, T])
    MV = kb.sb("MV", [128, DEPTH, 48, 2])
    phase_mod(kb, MV, cT, modw, modb)
    hin = [H0, HA, HB, HA]
    hout = [HA, HB, HA, OUT]
    for l in range(nlayers):
        kind, occ = l % 3, l // 3
        Hc = hin[l]
        Ho = hout[l] if l < nlayers - 1 else OUT
        if kind == 0:
            phase_s5(kb, l, occ, occ % 2 == 1, MV, Hc, Gs, Yb, *s5in, ident, Lx, Cx)
        elif kind == 1:
            phase_mlstm(kb, l, MV, Hc, Yb, *big[:3], *big[3:6], GT, big[6], *mlin, ident, tri, Lx, Cx)
        else:
            phase_conf(kb, l, MV, Hc, Yb, *cvin, ident, Lx, Cx)
        phase_p2(kb, l, MV, Hc, Yb, Ho, p2vec, wg, wu, wd, Lx, Cx)
    return kb.finish()


def host_maps(inp, nb):
    shared = {}
    shared.update(host_p2(inp))
    shared.update(host_s5(inp))
    shared.update(host_ml(inp))
    shared.update(host_conf(inp))
    maps = []
    for b in range(nb):
        m = dict(shared)
        m.update(host_common(inp, b))
        m["hT"] = to_fm(np.concatenate([inp["ctx"][b], inp["x"][b]], axis=0))
        maps.append(m)
    return maps


def kernel(**inputs):
    inp = {k: np.asarray(v, dtype=np.float32) for k, v in inputs.items()}
    nb, Lx, _ = inp["x"].shape
    Cx = inp["ctx"].shape[1]
    nc = build_mega(Lx, Cx)
    maps = host_maps(inp, nb)
    res = run_bass_kernel_spmd(nc, maps, core_ids=list(range(nb)))
    out = np.stack([from_fm(res.results[b]["oT"])[Cx:] for b in range(nb)], axis=0)
    return out.astype(np.float32)

def build_test(phase, Lx, Cx, l):
    kb = KB()
    T = Cx + Lx
    cT = kb.din("cT", [128, 8, 2])
    modw = kb.din("modw", [DEPTH, 128, 8, 6144])
    modb = kb.din("modb", [128, DEPTH, 48])
    H = kb.dram("hT", [128, 8, T], kind="ExternalInput")
    O = kb.dram("oT", [128, 8, T], kind="ExternalOutput")
    MV = kb.sb("MV", [128, DEPTH, 48, 2])
    phase_mod(kb, MV, cT, modw, modb)
    if phase == "p2":
        Y = kb.dram("yT", [128, 8, T], kind="ExternalInput")
        p2vec = kb.din("p2vec", [DEPTH, 128, NV_P2])
        wg = kb.din("wg", [DEPTH, NF, 128, 8, 128])
        wu = kb.din("wu", [DEPTH, NF, 128, 8, 128])
        wd = kb.din("wd", [DEPTH, 8, 128, NF, 128])
        phase_p2(kb, l, MV, H, Y, O, p2vec, wg, wu, wd, Lx, Cx)
    elif phase == "s5":
        occ = l // 3
        G = kb.dram("Gs", [128, 8, T])
        phase_s5(kb, l, occ, occ % 2 == 1, MV, H, G, O,
                 kb.din("s5par", [2, 2, 128, 3, 32]), kb.din("s5bre", [2, 2, 128, 32, 128]),
                 kb.din("s5bim", [2, 2, 128, 32, 128]), kb.din("s5cre", [2, 2, 128, 32, 128]),
                 kb.din("s5cim", [2, 2, 128, 32, 128]), kb.din("s5vec", [2, 128, 24]),
                 kb.din("wglu", [2, 128, 8, 2048]), kb.din("ident", [128, 128]), Lx, Cx)
    elif phase == "ml":
        XM, XC, Z, QT, KT, VT, HH = [kb.dram(nm, [128, 16, T]) for nm in ("XM", "XC", "Z", "QT", "KT", "VT", "HH")]
        GT = kb.dram("GT", [16, T])
        phase_mlstm(kb, l, MV, H, O, XM, XC, Z, QT, KT, VT, GT, HH,
                    kb.din("ml_wup", [32, 128, 8, 128]), kb.din("ml_wq", [16, 128, 16, 128]),
                    kb.din("ml_wk", [16, 128, 16, 128]), kb.din("ml_wv", [16, 128, 16, 128]),
                    kb.din("ml_wg", [128, 3, 16, 16]), kb.din("ml_bg", [16, 1]), kb.din("mlvec", [128, NV_ML]),
                    kb.din("ml_wdown", [128, 16, 1024]), kb.din("ident", [128, 128]), kb.din("tri", [128, 128]), Lx, Cx)
    elif phase == "conf":
        cvvec = kb.din("cvvec", [128, NV_CV])
        cv_win = kb.din("cv_win", [128, 8, 2048])
        cv_wout = kb.din("cv_wout", [128, 8, 1024])
        phase_conf(kb, l, MV, H, O, cvvec, cv_win, cv_wout, kb.din("ident", [128, 128]), Lx, Cx)
    return kb.finish()


def host_common(inp, b):
    cc = np.stack([inp["c"][b], inp["c_ctx"]], axis=1)
    m = {}
    m["cT"] = np.ascontiguousarray(cc.reshape(8, 128, 2).transpose(1, 0, 2))
    m["modw"] = np.ascontiguousarray(inp["mod_w"].reshape(DEPTH, 8, 128, 6144).transpose(0, 2, 1, 3))
    m["modb"] = np.ascontiguousarray(inp["mod_b"].reshape(DEPTH, 48, 128).transpose(2, 0, 1))
    return m


def host_p2(inp):
    m = {}
    vecs = []
    for l in range(DEPTH):
        cols = [fm(inp["post_ln_g"][l, 0]), fm(inp["post_ln_b"][l, 0]), fm(inp["post_ln_g"][l, 1]),
                fm(inp["post_ln_b"][l, 1]), fm(inp["ffn_conv_b"][l]),
                np.ascontiguousarray(inp["ffn_conv_w"][l].reshape(9, NF, 128).transpose(2, 1, 0)).reshape(128, NF * 9)]
        vecs.append(np.concatenate(cols, axis=1))
    m["p2vec"] = np.ascontiguousarray(np.stack(vecs, 0).astype(np.float32))
    m["wg"] = np.ascontiguousarray(inp["ffn_w_gate"].reshape(DEPTH, 8, 128, NF, 128).transpose(0, 3, 2, 1, 4))
    m["wu"] = np.ascontiguousarray(inp["ffn_w_up"].reshape(DEPTH, 8, 128, NF, 128).transpose(0, 3, 2, 1, 4))
    m["wd"] = np.ascontiguousarray(inp["ffn_w_down"].reshape(DEPTH, NF, 128, 8, 128).transpose(0, 3, 2, 1, 4))
    return m


def host_conf(inp):
    m = {}
    cols = [fm(inp["cv_b_in"][0]),
            np.ascontiguousarray(inp["cv_dw_w"][0].reshape(31, 8, 128).transpose(2, 0, 1)).reshape(128, 248),
            fm(inp["cv_dw_b"][0]), fm(inp["cv_ln_g"][0]), fm(inp["cv_ln_b"][0]), fm(inp["cv_b_out"][0])]
    m["cvvec"] = np.ascontiguousarray(np.concatenate(cols, axis=1).astype(np.float32))
    m["cv_win"] = np.ascontiguousarray(inp["cv_w_in"][0].reshape(8, 128, 2048).transpose(1, 0, 2))
    m["cv_wout"] = np.ascontiguousarray(inp["cv_w_out"][0].reshape(8, 128, 1024).transpose(1, 0, 2))
    m["ident"] = np.eye(128, dtype=np.float32)
    return m
```

```python
import contextlib
import numpy as np
import concourse.bass as bass
import concourse.mybir as mybir
from concourse.bass_utils import run_bass_kernel_spmd

F32 = mybir.dt.float32
BF16 = mybir.dt.bfloat16
AF = mybir.ActivationFunctionType
ALU = mybir.AluOpType

D = 1024
B = 4
L = 8192
DEPTH = 4
GW = 64
CTX = 256
ALPHA = (2 * DEPTH) ** 0.25
EPS = 1e-5
FF = 2816
NF = FF // 128
NCORES = 8


class Buf:
    def __init__(self, t, track=True, is_out=False):
        self.t = t
        self.w = None
        self.r = {}
        self.track = track
        self.is_out = is_out

    def __getitem__(self, idx):
        return self.t[idx]


class KB:
    def __init__(self):
        self.nc = bass.Bass("TRN2", target_bir_lowering=False)
        self.es = contextlib.ExitStack()
        nc = self.nc
        self.engs = {}
        for name, e in [("pe", nc.tensor), ("act", nc.scalar), ("dve", nc.vector),
                        ("pool", nc.gpsimd), ("sp", nc.sync)]:
            sem = self.es.enter_context(nc.semaphore("sem_" + name))
            self.engs[name] = dict(e=e, sem=sem, n=0, waited={}, name=name)
        self.dsems = [self.es.enter_context(nc.semaphore("dsem%d" % i)) for i in range(16)]
        self.dval = [0] * 16
        self.di = 0
        self.out_tok = []
        self.nps = 0

    def sb(self, name, shape, dt=F32):
        self.nps += 1
        return Buf(self.es.enter_context(self.nc.sbuf_tensor("%s_%d" % (name, self.nps), list(shape), dt)))

    def ps(self, name, shape=(128, 512), dt=F32):
        self.nps += 1
        return Buf(self.es.enter_context(self.nc.psum_tensor("%s_%d" % (name, self.nps), list(shape), dt)))

    def din(self, name, shape, dt=F32):
        return self.nc.dram_tensor(name, list(shape), dt, kind="ExternalInput").ap()

    def dout(self, name, shape, dt=F32):
        return self.nc.dram_tensor(name, list(shape), dt, kind="ExternalOutput").ap()

    def _wait(self, E, deps):
        for tok in deps:
            if tok is None:
                continue
            sem, val, src = tok
            if src == "pe" and E["name"] == "pe":
                continue
            key = id(sem)
            if E["waited"].get(key, 0) < val:
                E["e"].wait_ge(sem, val)
                E["waited"][key] = val

    def _deps(self, reads, writes):
        deps = []
        for b in reads:
            if b.track:
                deps.append(b.w)
        for b in writes:
            if b.track:
                deps.append(b.w)
                deps.extend(b.r.values())
        return deps

    def dram(self, name, shape, kind="Internal", dt=F32):
        t = self.nc.dram_tensor(name, list(shape), dt, kind=kind).ap()
        return Buf(t, track=False, is_out=(kind == "ExternalOutput"))

    def op(self, eng, fn, reads=(), writes=()):
        E = self.engs[eng]
        self._wait(E, self._deps(reads, writes))
        ins = fn(E["e"])
        E["n"] += 1
        ins.then_inc(E["sem"], 1)
        tok = (E["sem"], E["n"], eng)
        for b in writes:
            if b.track:
                b.w = tok
                b.r = {}
        for b in reads:
            if b.track:
                b.r[eng] = tok
        return ins

    def dma(self, out, in_, reads=(), writes=(), q="sp", is_out=False):
        E = self.engs[q]
        self._wait(E, self._deps(reads, writes))
        i = self.di
        self.di = (self.di + 1) % len(self.dsems)
        self.dval[i] += 16
        E["e"].dma_start(out=out, in_=in_).then_inc(self.dsems[i], 16)
        tok = (self.dsems[i], self.dval[i], "dma%d" % i)
        for b in writes:
            if b.track:
                b.w = tok
                b.r = {}
        for b in reads:
            if b.track:
                b.r["dma%d_%d" % (i, self.dval[i])] = tok
        if is_out:
            self.out_tok.append(tok)

    def finish(self):
        barrier(self)
        self.es.close()
        return self.nc


def mm(kb, ps, lhsT, rhs, start, stop, reads, ps_ap=None):
    out = ps_ap if ps_ap is not None else ps.t[:]
    kb.op("pe", lambda e: e.matmul(out, lhsT, rhs, start=start, stop=stop), reads=reads, writes=[ps])


def barrier(kb):
    toks = [(E["sem"], E["n"], "bar") for E in kb.engs.values() if E["n"] > 0]
    toks += [(kb.dsems[i], kb.dval[i], "bar") for i in range(len(kb.dsems)) if kb.dval[i] > 0]
    for E in kb.engs.values():
        kb._wait(E, toks)


class Phase:
    def __init__(self, kb):
        self.kb = kb

    def __enter__(self):
        self.outer = self.kb.es
        self.kb.es = contextlib.ExitStack()
        return self

    def __exit__(self, *a):
        barrier(self.kb)
        self.kb.es.close()
        self.kb.es = self.outer


def fm(vec):
    return np.ascontiguousarray(vec.reshape(-1, 128).T)


def to_fm(a):
    t, d = a.shape
    return np.ascontiguousarray(a.T.reshape(d // 128, 128, t).transpose(1, 0, 2))


def from_fm(a):
    p, k, t = a.shape
    return np.ascontiguousarray(a.transpose(1, 0, 2).reshape(k * 128, t).T)


def ts(kb, eng, out, in0, s1, s2, op0, op1, reads, writes):
    if s2 is None:
        kb.op(eng, lambda e: e.tensor_scalar(out=out, in0=in0, scalar1=s1, scalar2=None, op0=op0), reads, writes)
    else:
        kb.op(eng, lambda e: e.tensor_scalar(out=out, in0=in0, scalar1=s1, scalar2=s2, op0=op0, op1=op1), reads, writes)


def stt(kb, out, in0, sc, in1, op0, op1, reads, writes):
    kb.op("dve", lambda e: e.scalar_tensor_tensor(out=out, in0=in0, scalar=sc, in1=in1, op0=op0, op1=op1), reads, writes)


def tt(kb, out, in0, in1, op, reads, writes, eng="dve"):
    kb.op(eng, lambda e: e.tensor_tensor(out=out, in0=in0, in1=in1, op=op), reads, writes)


def act(kb, out, in_, func, reads, writes, bias=None, scale=None):
    kw = {}
    if bias is not None:
        kw["bias"] = bias
    if scale is not None:
        kw["scale"] = scale
    kb.op("act", lambda e: e.activation(out=out, in_=in_, func=func, **kw), reads, writes)


class LNHelper:
    def __init__(self, kb, nmax):
        self.kb = kb
        self.ones = kb.sb("ln_ones", [128, 128])
        self.sq = kb.sb("ln_sq", [128, nmax])
        self.mean = kb.sb("ln_mean", [128, nmax])
        self.m2 = kb.sb("ln_m2", [128, nmax])
        self.rstd = kb.sb("ln_rstd", [128, nmax])
        self.tmp = kb.sb("ln_tmp", [128, nmax])
        self.S1 = kb.ps("ln_S1")
        self.S2 = kb.ps("ln_S2")
        kb.op("dve", lambda e: e.memset(self.ones.t[:], 1.0), writes=[self.ones])

    def __call__(self, src, lo, n, dst, dlo, gvec, bvec, vbuf, func=None):
        kb = self.kb
        sq, mean, m2, rstd, tmp, S1, S2, ones = self.sq, self.mean, self.m2, self.rstd, self.tmp, self.S1, self.S2, self.ones
        for k in range(8):
            act(kb, sq.t[:, :n], src.t[:, k, lo:lo + n], AF.Square, [src], [sq])
            mm(kb, S1, ones.t[:], src.t[:, k, lo:lo + n], k == 0, k == 7, [ones, src], ps_ap=S1.t[:, :n])
            mm(kb, S2, ones.t[:], sq.t[:, :n], k == 0, k == 7, [ones, sq], ps_ap=S2.t[:, :n])
        kb.op("act", lambda e: e.mul(mean.t[:, :n], S1.t[:, :n], 1.0 / D), reads=[S1], writes=[mean])
        tt(kb, m2.t[:, :n], mean.t[:, :n], mean.t[:, :n], ALU.mult, [mean], [m2])
        stt(kb, m2.t[:, :n], S2.t[:, :n], 1.0 / D, m2.t[:, :n], ALU.mult, ALU.subtract, [S2, m2], [m2])
        ts(kb, "dve", m2.t[:, :n], m2.t[:, :n], EPS, None, ALU.add, None, [m2], [m2])
        act(kb, m2.t[:, :n], m2.t[:, :n], AF.Sqrt, [m2], [m2])
        kb.op("dve", lambda e: e.reciprocal(out=rstd.t[:, :n], in_=m2.t[:, :n]), reads=[m2], writes=[rstd])
        for k in range(8):
            tt(kb, tmp.t[:, :n], src.t[:, k, lo:lo + n], mean.t[:, :n], ALU.subtract, [src, mean], [tmp])
            tt(kb, tmp.t[:, :n], tmp.t[:, :n], rstd.t[:, :n], ALU.mult, [tmp, rstd], [tmp])
            if func is None:
                ts(kb, "dve", dst.t[:, k, dlo:dlo + n], tmp.t[:, :n], gvec(k), bvec(k), ALU.mult, ALU.add,
                   [tmp, vbuf], [dst])
            else:
                ts(kb, "dve", tmp.t[:, :n], tmp.t[:, :n], gvec(k), bvec(k), ALU.mult, ALU.add, [tmp, vbuf], [tmp])
                act(kb, dst.t[:, k, dlo:dlo + n], tmp.t[:, :n], func, [tmp], [dst])


def phase_mod(kb, MV, cT, modw, modb):
    with Phase(kb):
        cb = kb.sb("m_cb", [128, 8, 2])
        sc = kb.sb("m_sc", [128, 8, 2])
        bb = kb.sb("m_bb", [128, DEPTH, 48])
        wb = [kb.sb("m_wb%d" % i, [128, 8, 512]) for i in range(2)]
        pss = [kb.ps("m_ps%d" % i) for i in range(2)]
        kb.dma(cb.t[:], cT, writes=[cb])
        kb.dma(bb.t[:], modb, writes=[bb])
        act(kb, sc.t[:], cb.t[:], AF.Silu, [cb], [sc])
        n = 0
        for l in range(DEPTH):
            for c in range(12):
                w = wb[n % 2]
                kb.dma(w.t[:], modw[l, :, :, c * 512:(c + 1) * 512], writes=[w])
                for f4 in range(4):
                    ft = c * 4 + f4
                    ps = pss[ft % 2]
                    for k in range(8):
                        mm(kb, ps, w.t[:, k, f4 * 128:(f4 + 1) * 128], sc.t[:, k, :], k == 0, k == 7, [w, sc],
                           ps_ap=ps.t[:, 0:2])
                    j = ft // 8
                    ts(kb, "dve", MV.t[:, l, ft, :], ps.t[:, 0:2], bb.t[:, l, ft:ft + 1],
                       1.0 if j in (1, 2, 4, 5) else 0.0, ALU.add, ALU.add, [ps, bb], [MV])
                n += 1


NV_P2 = 32 + NF + NF * 9


def phase_p2(kb, l, MV, H, Y, HO, p2vec, wg, wu, wd, Lx, Cx):
    rows = Lx // GW
    RT = 8
    with Phase(kb):
        vb = kb.sb("p_vb", [128, NV_P2])
        N1M = (RT + 2) * GW
        hb = kb.sb("p_hb", [128, 8, N1M])
        yb = kb.sb("p_yb", [128, 8, N1M])
        u2 = kb.sb("p_u2", [128, 8, N1M], BF16)
        tmp = kb.sb("p_tmp", [128, N1M])
        hid = kb.sb("p_hid", [128, NF, 512], BF16)
        wgb = [kb.sb("p_wgb%d" % i, [128, 8, 128], BF16) for i in range(4)]
        wub = [kb.sb("p_wub%d" % i, [128, 8, 128], BF16) for i in range(4)]
        wdb = [kb.sb("p_wdb%d" % i, [128, NF, 128], BF16) for i in range(3)]
        gpads = [kb.sb("p_gpad%d" % i, [128, RT + 2, 66]) for i in range(2)]
        gpcs = [kb.sb("p_gpc%d" % i, [128, 514]) for i in range(2)]
        accs = [kb.sb("p_acc%d" % i, [128, 512]) for i in range(2)]
        accb = kb.sb("p_accb", [128, 512])
        hgs = [kb.sb("p_hg%d" % i, [128, 512]) for i in range(2)]
        z2 = kb.sb("p_z2", [128, 8, 512])
        ln = LNHelper(kb, 512)
        GA = kb.ps("p_GA")
        GB = kb.ps("p_GB")
        Up = [kb.ps("p_Up%d" % i) for i in range(2)]
        Fp = [kb.ps("p_Fp%d" % i) for i in range(2)]
        kb.dma(vb.t[:], p2vec[l], writes=[vb])
        WGb = kb.dram("p2wgb_%d" % l, [NF, 128, 8, 128], dt=BF16)
        WUb = kb.dram("p2wub_%d" % l, [NF, 128, 8, 128], dt=BF16)
        WDb = kb.dram("p2wdb_%d" % l, [8, 128, NF, 128], dt=BF16)
        for f in range(NF):
            for src_, dst_, stg in ((wg, WGb, wgb), (wu, WUb, wub)):
                kb.dma(stg[f % 2].t[:], src_[l, f], writes=[stg[f % 2]], q="pool")
                kb.dma(dst_.t[f], stg[f % 2].t[:], reads=[stg[f % 2]])
        for j in range(8):
            kb.dma(wdb[j % 2].t[:], wd[l, j], writes=[wdb[j % 2]], q="pool")
            kb.dma(WDb.t[j], wdb[j % 2].t[:], reads=[wdb[j % 2]])
        barrier(kb)
        for g_ in gpads + gpcs:
            kb.op("dve", lambda e: e.memset(g_.t[:], 0.0), writes=[g_])
        CB0, CW0 = 32, 32 + NF

        def vcol(i):
            return vb.t[:, i:i + 1]

        def mv(j, k, s):
            return MV.t[:, l, j * 8 + k, s:s + 1]

        wcount = [0]

        def tile(t0, n1, i0, n2, s, gp0, zero_rows, nxt=None, y_loaded=False):
            is_ctx = s == 1
            kb.dma(hb.t[:, :, :n1], H.t[:, :, t0:t0 + n1], writes=[hb])
            if not y_loaded:
                kb.dma(yb.t[:, :, :n1], Y.t[:, :, t0:t0 + n1], writes=[yb])
            for k in range(8):
                ts(kb, "dve", tmp.t[:, :n1], yb.t[:, k, :n1], mv(2, k, s), None, ALU.mult, None, [yb, MV], [tmp])
                stt(kb, yb.t[:, k, :n1], hb.t[:, k, :n1], ALPHA, tmp.t[:, :n1], ALU.mult, ALU.add, [hb, tmp], [yb])
            for c0 in range(0, n1, 512):
                cn = min(512, n1 - c0)
                ln(yb, c0, cn, hb, c0, lambda k: vcol(k), lambda k: vcol(8 + k), vb)
            if nxt is not None:
                kb.dma(yb.t[:, :, :nxt[1]], Y.t[:, :, nxt[0]:nxt[0] + nxt[1]], writes=[yb])
            for k in range(8):
                ts(kb, "dve", u2.t[:, k, :n1], hb.t[:, k, :n1], mv(4, k, s), mv(3, k, s), ALU.mult, ALU.add,
                   [hb, MV], [u2])
            for zr in zero_rows:
                for g_ in gpads:
                    kb.op("dve", lambda e: e.memset(g_.t[:, zr, :], 0.0), writes=[g_])
            na = min(n1, 512)
            nb_ = n1 - na

            def load_gu(f):
                kb.dma(wgb[f % 4].t[:], WGb.t[f], writes=[wgb[f % 4]], q="pool")
                kb.dma(wub[f % 4].t[:], WUb.t[f], writes=[wub[f % 4]], q="pool")

            def load_d(j):
                kb.dma(wdb[j % 3].t[:], WDb.t[j], writes=[wdb[j % 3]], q="pool")

            def stage_a(f):
                sl = f % 2
                wq_, wu_ = wgb[f % 4], wub[f % 4]
                U = Up[sl]
                gp_ = gpads[sl]
                for k in range(8):
                    mm(kb, GA, wq_.t[:, k, :], u2.t[:, k, :na], k == 0, k == 7, [wq_, u2], ps_ap=GA.t[:, :na])
                if nb_ > 0:
                    for k in range(8):
                        mm(kb, GB, wq_.t[:, k, :], u2.t[:, k, na:n1], k == 0, k == 7, [wq_, u2], ps_ap=GB.t[:, :nb_])
                for k in range(8):
                    mm(kb, U, wu_.t[:, k, :], u2.t[:, k, i0:i0 + n2], k == 0, k == 7, [wu_, u2],
                       ps_ap=U.t[:, :n2])
                if not is_ctx:
                    ra = na // GW
                    kb.op("act", lambda e: e.copy(out=gp_.t[:, gp0:gp0 + ra, 1:65],
                                                  in_=GA.t[:, :na].rearrange("p (r c) -> p r c", c=GW)),
                          reads=[GA], writes=[gp_])
                    if nb_ > 0:
                        rb = nb_ // GW
                        kb.op("act", lambda e: e.copy(out=gp_.t[:, gp0 + ra:gp0 + ra + rb, 1:65],
                                                      in_=GB.t[:, :nb_].rearrange("p (r c) -> p r c", c=GW)),
                              reads=[GB], writes=[gp_])
                else:
                    kb.op("act", lambda e: e.copy(out=gpcs[sl].t[:, 1:1 + n1], in_=GA.t[:, :n1]), reads=[GA], writes=[gpcs[sl]])

            def stage_b(f):
                sl = f % 2
                cw = CW0 + f * 9
                ac, hg_ = accs[sl], hgs[sl]
                acb = accb
                if not is_ctx:
                    gp_ = gpads[sl]
                    a3 = ac.t[:, :n2].rearrange("p (r c) -> p r c", c=GW)
                    b3 = acb.t[:, :n2].rearrange("p (r c) -> p r c", c=GW)
                    taps = [(kh, kw) for kh in range(3) for kw in range(3)]
                    for ti_, (kh, kw) in enumerate(taps):
                        src = gp_.t[:, kh:kh + RT, kw:kw + 64]
                        wc = vcol(cw + kh * 3 + kw)
                        dst3, dbuf = (a3, ac) if ti_ % 2 == 0 else (b3, acb)
                        if ti_ < 2:
                            ts(kb, "dve", dst3, src, wc, None, ALU.mult, None, [gp_, vb], [dbuf])
                        else:
                            stt(kb, dst3, src, wc, dst3, ALU.mult, ALU.add, [gp_, vb, dbuf], [dbuf])
                    tt(kb, ac.t[:, :n2], ac.t[:, :n2], acb.t[:, :n2], ALU.add, [ac, acb], [ac])
                else:
                    gq = gpcs[sl]
                    for kw in range(3):
                        src = gq.t[:, kw:kw + n1]
                        wc = vcol(cw + 3 + kw)
                        if kw == 0:
                            ts(kb, "dve", ac.t[:, :n1], src, wc, None, ALU.mult, None, [gq, vb], [ac])
                        else:
                            stt(kb, ac.t[:, :n1], src, wc, ac.t[:, :n1], ALU.mult, ALU.add, [gq, vb, ac], [ac])
                act(kb, hg_.t[:, :n2], ac.t[:, :n2], AF.Gelu, [ac, vb], [hg_], bias=vcol(CB0 + f))

            def stage_c(f):
                sl = f % 2
                tt(kb, hid.t[:, f, :n2], hgs[sl].t[:, :n2], Up[sl].t[:, :n2], ALU.mult, [hgs[sl], Up[sl]], [hid])

            load_gu(0)
            load_gu(1)
            load_gu(2)
            load_d(0)
            load_d(1)
            stage_a(0)
            stage_a(1)
            stage_b(0)
            for f in range(NF):
                if f + 3 < NF:
                    load_gu(f + 3)
                if f + 1 < NF:
                    stage_b(f + 1)
                stage_c(f)
                if f + 2 < NF:
                    stage_a(f + 2)
            for j in range(8):
                sl = j % 2
                if j + 2 < 8:
                    load_d(j + 2)
                wd_ = wdb[j % 3]
                Fq = Fp[sl]
                for f in range(NF):
                    mm(kb, Fq, wd_.t[:, f, :], hid.t[:, f, :n2], f == 0, f == NF - 1, [wd_, hid],
                       ps_ap=Fq.t[:, :n2])
                ts(kb, "dve", tmp.t[:, :n2], Fq.t[:, :n2], mv(5, j, s), None, ALU.mult, None, [Fq, MV], [tmp])
                stt(kb, z2.t[:, j, :n2], hb.t[:, j, i0:i0 + n2], ALPHA, tmp.t[:, :n2], ALU.mult, ALU.add,
                    [hb, tmp], [z2])
            ln(z2, 0, n2, z2, 0, lambda k: vcol(16 + k), lambda k: vcol(24 + k), vb)
            kb.dma(HO.t[:, :, t0 + i0:t0 + i0 + n2], z2.t[:, :, :n2], reads=[z2])

        ntile = rows // RT
        tiles = []
        for ti in range(ntile):
            rlo = max(RT * ti - 1, 0)
            rhi = min(RT * ti + RT + 1, rows)
            t0 = Cx + rlo * GW
            n1 = (rhi - rlo) * GW
            i0 = (RT * ti - rlo) * GW
            gp0 = rlo - (RT * ti - 1)
            zr = ([0] if ti == 0 else []) + ([RT + 1] if ti == ntile - 1 else [])
            tiles.append((t0, n1, i0, RT * GW, 0, gp0, zr))
        assert Cx <= 512
        tiles.append((0, Cx, 0, Cx, 1, 0, []))
        for i, tp_ in enumerate(tiles):
            nxt = (tiles[i + 1][0], tiles[i + 1][1]) if i + 1 < len(tiles) else None
            tile(*tp_, nxt=nxt, y_loaded=(i > 0))


NV_CV = 16 + 31 * 8 + 8 + 8 + 8 + 8


def phase_conf(kb, l, MV, H, Y, cvvec, cv_win, cv_wout, ident, Lx, Cx):
    with Phase(kb):
        vb = kb.sb("c_vb", [128, NV_CV])
        win = kb.sb("c_win", [128, 8, 2048], BF16)
        wout = kb.sb("c_wout", [128, 8, 1024], BF16)
        NW = 286
        hb = kb.sb("c_hb", [128, 8, NW])
        ub = kb.sb("c_ub", [128, 8, NW], BF16)
        sg = kb.sb("c_sg", [128, NW])
        hp = kb.sb("c_hp", [128, 8, NW], BF16)
        idn = kb.sb("c_idn", [128, 128])
        dg = kb.sb("c_dg", [128, 248, 128], BF16)
        Cp = [kb.ps("c_Cp%d" % i) for i in range(2)]
        cv = kb.sb("c_cv", [128, 8, 256])
        hm = kb.sb("c_hm", [128, 8, 256], BF16)
        ob = kb.sb("c_ob", [128, 8, 256])
        ln = LNHelper(kb, 256)
        Ap = [kb.ps("c_Ap%d" % i) for i in range(2)]
        Bp = [kb.ps("c_Bp%d" % i) for i in range(2)]
        kb.dma(vb.t[:], cvvec, writes=[vb])
        for k in range(8):
            kb.dma(win.t[:, k, :], cv_win[:, k, :], writes=[win], q="pool")
            kb.dma(wout.t[:, k, :], cv_wout[:, k, :], writes=[wout], q="pool")
        BI0, DW0, DB0, LG0, LB0, BO0 = 0, 16, 16 + 248, 16 + 256, 16 + 264, 16 + 272
        kb.dma(idn.t[:], ident, writes=[idn])
        for i in range(248):
            ts(kb, "dve", dg.t[:, i, :], idn.t[:], vb.t[:, DW0 + i:DW0 + i + 1], None, ALU.mult, None, [idn, vb], [dg])

        def vcol(i):
            return vb.t[:, i:i + 1]

        def tile(seq0, seqn, p0, n2, s):
            lo = max(p0 - 15, 0)
            hi = min(p0 + n2 + 15, seqn)
            n1 = hi - lo
            off = lo - (p0 - 15)
            kb.dma(hb.t[:, :, :n1], H.t[:, :, seq0 + lo:seq0 + hi], reads=[H], writes=[hb])
            if n1 < n2 + 30:
                kb.op("dve", lambda e: e.memset(hp.t[:], 0.0), writes=[hp])
            for k in range(8):
                ts(kb, "dve", ub.t[:, k, :n1], hb.t[:, k, :n1], MV.t[:, l, 8 + k, s:s + 1], MV.t[:, l, k, s:s + 1],
                   ALU.mult, ALU.add, [hb, MV], [ub])
            for j in range(8):
                A, Bq = Ap[j % 2], Bp[j % 2]
                for k in range(8):
                    mm(kb, A, win.t[:, k, j * 128:(j + 1) * 128], ub.t[:, k, :n1], k == 0, k == 7, [win, ub],
                       ps_ap=A.t[:, :n1])
                for k in range(8):
                    mm(kb, Bq, win.t[:, k, 1024 + j * 128:1024 + (j + 1) * 128], ub.t[:, k, :n1], k == 0, k == 7,
                       [win, ub], ps_ap=Bq.t[:, :n1])
                act(kb, sg.t[:, :n1], Bq.t[:, :n1], AF.Sigmoid, [Bq, vb], [sg], bias=vcol(BI0 + 8 + j))
                stt(kb, hp.t[:, j, off:off + n1], A.t[:, :n1], vcol(BI0 + j), sg.t[:, :n1], ALU.add, ALU.mult,
                    [A, vb, sg], [hp])
            for k in range(8):
                Cq = Cp[k % 2]
                for tap in range(31):
                    mm(kb, Cq, dg.t[:, tap * 8 + k, :], hp.t[:, k, tap:tap + n2], tap == 0, tap == 30, [dg, hp],
                       ps_ap=Cq.t[:, :n2])
                ts(kb, "dve", cv.t[:, k, :n2], Cq.t[:, :n2], vcol(DB0 + k), None, ALU.add, None, [Cq, vb], [cv])
            ln(cv, 0, n2, hm, 0, lambda k: vcol(LG0 + k), lambda k: vcol(LB0 + k), vb, func=AF.Silu)
            for j in range(8):
                A = Ap[j % 2]
                for k in range(8):
                    mm(kb, A, wout.t[:, k, j * 128:(j + 1) * 128], hm.t[:, k, :n2], k == 0, k == 7, [wout, hm],
                       ps_ap=A.t[:, :n2])
                ts(kb, "dve", ob.t[:, j, :n2], A.t[:, :n2], vcol(BO0 + j), None, ALU.add, None, [A, vb], [ob])
            kb.dma(Y.t[:, :, seq0 + p0:seq0 + p0 + n2], ob.t[:, :, :n2], reads=[ob], writes=[Y])

        for p0 in range(0, Lx, 256):
            tile(Cx, Lx, p0, 256, 0)
        for p0 in range(0, Cx, 256):
            tile(0, Cx, p0, min(256, Cx - p0), 1)


I32 = mybir.dt.int32
PI = float(np.pi)
SEG = 128


def phase_s5(kb, l, occ, col_major, MV, H, G, Y, s5par, s5bre, s5bim, s5cre, s5cim, s5vec, wglu, ident, Lx, Cx):
    T = Cx + Lx
    rows = Lx // GW
    with Phase(kb):
        idn = kb.sb("s_idn", [128, 128])
        vb = kb.sb("s_vb", [128, 24])
        kb.dma(idn.t[:], ident, writes=[idn])
        kb.dma(vb.t[:], s5vec[occ], writes=[vb])
        bufA = kb.sb("s_A", [128, T])
        bufB = kb.sb("s_B", [128, T])
        bufC = kb.sb("s_C", [128, T])
        names = ["lre", "lim", "dt", "mag", "ang", "cs", "sn", "cr", "ci", "nci", "t0", "t1", "t2"]
        cf = {n: kb.sb("s_cf_" + n, [128, 32]) for n in names}
        pt = kb.sb("s_pt", [128, 3, 32])
        ti32 = kb.sb("s_ti32", [128, 32], I32)
        Bre = kb.sb("s_Bre", [128, 4, 128])
        Bim = kb.sb("s_Bim", [128, 4, 128])
        Xre = kb.sb("s_Xre", [128, 4, 128])
        Xim = kb.sb("s_Xim", [128, 4, 128])
        WBre = kb.sb("s_WBre", [128, 4, 128])
        WBim = kb.sb("s_WBim", [128, 4, 128])
        Cre = kb.sb("s_Cre", [128, 4, 128])
        Cnim = kb.sb("s_Cnim", [128, 4, 128])
        Tc = kb.sb("s_Tc", [128, 4, 512])
        Ts = kb.sb("s_Ts", [128, 4, 512])
        Rt = kb.sb("s_Rt", [128, 4, 512])
        Ec = kb.sb("s_Ec", [128, 4])
        Es = kb.sb("s_Es", [128, 4])
        car = kb.sb("s_car", [128, 4, 2])
        ctmp = kb.sb("s_ctmp", [128, 2])
        ctmp2 = kb.sb("s_ctmp2", [128, 2])
        m1 = kb.sb("s_m1", [128, 512])
        m2 = kb.sb("s_m2", [128, 512])
        m3 = kb.sb("s_m3", [128, 512])
        m4 = kb.sb("s_m4", [128, 512])
        p1 = kb.sb("s_p1", [128, 512])
        p2 = kb.sb("s_p2", [128, 512])
        wres = [kb.sb("s_wre%d" % i, [128, 512]) for i in range(2)]
        wims = [kb.sb("s_wim%d" % i, [128, 512]) for i in range(2)]
        sre = [kb.sb("s_sre%d" % i, [128, 512]) for i in range(2)]
        sim = [kb.sb("s_sim%d" % i, [128, 512]) for i in range(2)]
        bur = [kb.ps("s_bur%d" % i) for i in range(2)]
        bui = [kb.ps("s_bui%d" % i) for i in range(2)]
        yps = [kb.ps("s_yps%d" % i) for i in range(2)]
        tps = kb.ps("s_tps")

        def c(n):
            return cf[n].t[:]

        def reduce_sin(dst, src_name, shift):
            ts(kb, "dve", c("t0"), c(src_name), shift, 1.0 / (2 * PI), ALU.add, ALU.mult, [cf[src_name]], [cf["t0"]])
            kb.op("dve", lambda e: e.tensor_copy(out=ti32.t[:], in_=c("t0")), [cf["t0"]], [ti32])
            kb.op("dve", lambda e: e.tensor_copy(out=c("t1"), in_=ti32.t[:]), [ti32], [cf["t1"]])
            ts(kb, "dve", c("t0"), c(src_name), shift, None, ALU.add, None, [cf[src_name]], [cf["t0"]])
            stt(kb, c("t0"), c("t1"), -2 * PI, c("t0"), ALU.mult, ALU.add, [cf["t1"], cf["t0"]], [cf["t0"]])
            ts(kb, "dve", c("t1"), c("t0"), PI, None, ALU.is_gt, None, [cf["t0"]], [cf["t1"]])
            stt(kb, c("t0"), c("t1"), -2 * PI, c("t0"), ALU.mult, ALU.add, [cf["t1"], cf["t0"]], [cf["t0"]])
            ts(kb, "dve", c("t1"), c("t0"), -PI, None, ALU.is_lt, None, [cf["t0"]], [cf["t1"]])
            stt(kb, c("t0"), c("t1"), 2 * PI, c("t0"), ALU.mult, ALU.add, [cf["t1"], cf["t0"]], [cf["t0"]])
            ts(kb, "dve", c("t0"), c("t0"), -PI, PI, ALU.max, ALU.min, [cf["t0"]], [cf["t0"]])
            act(kb, dst, c("t0"), AF.Sin, [cf["t0"]], [cf[dst_name[0]]])

        dst_name = [None]

        def coefs(d):
            kb.dma(pt.t[:], s5par[occ, d], writes=[pt])
            ts(kb, "dve", c("lre"), pt.t[:, 0, :], -1e-4, None, ALU.min, None, [pt], [cf["lre"]])
            kb.op("dve", lambda e: e.tensor_copy(out=c("lim"), in_=pt.t[:, 1, :]), [pt], [cf["lim"]])
            act(kb, c("dt"), pt.t[:, 2, :], AF.Exp, [pt], [cf["dt"]])
            tt(kb, c("t2"), c("lre"), c("dt"), ALU.mult, [cf["lre"], cf["dt"]], [cf["t2"]])
            act(kb, c("mag"), c("t2"), AF.Exp, [cf["t2"]], [cf["mag"]])
            tt(kb, c("ang"), c("lim"), c("dt"), ALU.mult, [cf["lim"], cf["dt"]], [cf["ang"]])
            dst_name[0] = "sn"
            reduce_sin(c("sn"), "ang", 0.0)
            dst_name[0] = "cs"
            reduce_sin(c("cs"), "ang", PI / 2)
            tt(kb, c("t0"), c("mag"), c("cs"), ALU.mult, [cf["mag"], cf["cs"]], [cf["t0"]])
            ts(kb, "dve", c("t0"), c("t0"), -1.0, None, ALU.add, None, [cf["t0"]], [cf["t0"]])
            tt(kb, c("t1"), c("mag"), c("sn"), ALU.mult, [cf["mag"], cf["sn"]], [cf["t1"]])
            tt(kb, c("t2"), c("lre"), c("lre"), ALU.mult, [cf["lre"]], [cf["t2"]])
            tt(kb, c("cr"), c("lim"), c("lim"), ALU.mult, [cf["lim"]], [cf["cr"]])
            tt(kb, c("t2"), c("t2"), c("cr"), ALU.add, [cf["t2"], cf["cr"]], [cf["t2"]])
            kb.op("dve", lambda e: e.reciprocal(out=c("t2"), in_=c("t2")), [cf["t2"]], [cf["t2"]])
            tt(kb, c("cr"), c("t0"), c("lre"), ALU.mult, [cf["t0"], cf["lre"]], [cf["cr"]])
            tt(kb, c("ci"), c("t1"), c("lim"), ALU.mult, [cf["t1"], cf["lim"]], [cf["ci"]])
            tt(kb, c("cr"), c("cr"), c("ci"), ALU.add, [cf["cr"], cf["ci"]], [cf["cr"]])
            tt(kb, c("cr"), c("cr"), c("t2"), ALU.mult, [cf["cr"], cf["t2"]], [cf["cr"]])
            tt(kb, c("ci"), c("t1"), c("lre"), ALU.mult, [cf["t1"], cf["lre"]], [cf["ci"]])
            tt(kb, c("nci"), c("t0"), c("lim"), ALU.mult, [cf["t0"], cf["lim"]], [cf["nci"]])
            tt(kb, c("ci"), c("ci"), c("nci"), ALU.subtract, [cf["ci"], cf["nci"]], [cf["ci"]])
            tt(kb, c("ci"), c("ci"), c("t2"), ALU.mult, [cf["ci"], cf["t2"]], [cf["ci"]])
            ts(kb, "dve", c("nci"), c("ci"), -1.0, None, ALU.mult, None, [cf["ci"]], [cf["nci"]])

        def col(n, q):
            return cf[n].t[:, q:q + 1]

        def unit_setup(d, k):
            kb.dma(Bre.t[:], s5bre[occ, d, :, 4 * k:4 * k + 4, :], writes=[Bre])
            kb.dma(Bim.t[:], s5bim[occ, d, :, 4 * k:4 * k + 4, :], writes=[Bim])
            kb.dma(Cre.t[:], s5cre[occ, d, :, 4 * k:4 * k + 4, :], writes=[Cre])
            kb.dma(Cnim.t[:], s5cim[occ, d, :, 4 * k:4 * k + 4, :], writes=[Cnim])
            ts(kb, "dve", Cnim.t[:], Cnim.t[:], -1.0, None, ALU.mult, None, [Cnim], [Cnim])
            for q4 in range(4):
                q = 4 * k + q4
                ts(kb, "dve", Xre.t[:, q4, :], Bre.t[:, q4, :], col("cr", q), None, ALU.mult, None, [Bre, cf["cr"]], [Xre])
                stt(kb, Xre.t[:, q4, :], Bim.t[:, q4, :], col("nci", q), Xre.t[:, q4, :], ALU.mult, ALU.add,
                    [Bim, cf["nci"], Xre], [Xre])
                ts(kb, "dve", Xim.t[:, q4, :], Bim.t[:, q4, :], col("cr", q), None, ALU.mult, None, [Bim, cf["cr"]], [Xim])
                stt(kb, Xim.t[:, q4, :], Bre.t[:, q4, :], col("ci", q), Xim.t[:, q4, :], ALU.mult, ALU.add,
                    [Bre, cf["ci"], Xim], [Xim])
                for X, W in ((Xre, WBre), (Xim, WBim)):
                    mm(kb, tps, X.t[:, q4, :], idn.t[:], True, True, [X, idn], ps_ap=tps.t[:, :128])
                    kb.op("act", lambda e: e.copy(out=W.t[:, q4, :], in_=tps.t[:, :128]), [tps], [W])
                kb.op("dve", lambda e: e.tensor_copy(out=Tc.t[:, q4, 0:1], in_=col("cs", q)), [cf["cs"]], [Tc])
                kb.op("dve", lambda e: e.tensor_copy(out=Ts.t[:, q4, 0:1], in_=col("sn", q)), [cf["sn"]], [Ts])
                m = 1
                while m < 512:
                    ec, es = Tc.t[:, q4, m - 1:m], Ts.t[:, q4, m - 1:m]
                    ts(kb, "dve", m1.t[:, :m], Ts.t[:, q4, 0:m], es, -1.0, ALU.mult, ALU.mult, [Ts], [m1])
                    stt(kb, Tc.t[:, q4, m:2 * m], Tc.t[:, q4, 0:m], ec, m1.t[:, :m], ALU.mult, ALU.add, [Tc, m1], [Tc])
                    ts(kb, "dve", m1.t[:, :m], Tc.t[:, q4, 0:m], es, None, ALU.mult, None, [Tc], [m1])
                    stt(kb, Ts.t[:, q4, m:2 * m], Ts.t[:, q4, 0:m], ec, m1.t[:, :m], ALU.mult, ALU.add, [Ts, m1], [Ts])
                    m *= 2
                kb.op("dve", lambda e: e.memset(Rt.t[:, q4, :], 1.0), writes=[Rt])
                ts(kb, "dve", Rt.t[:, q4, :], Rt.t[:, q4, :], col("mag", q), None, ALU.mult, None, [Rt, cf["mag"]], [Rt])
            kb.op("dve", lambda e: e.memset(car.t[:], 0.0), writes=[car])

        def unit_scan(uin, yout):
            items = []
            for ti, t0 in enumerate(range(0, T, 512)):
                for q4 in range(4):
                    items.append((ti, t0, min(512, T - t0), q4))

            def issue_bu(it):
                ti, t0, n, q4 = it
                br, bi = bur[q4 % 2], bui[q4 % 2]
                mm(kb, br, WBre.t[:, q4, :], uin.t[:, t0:t0 + n], True, True, [WBre, uin], ps_ap=br.t[:, :n])
                mm(kb, bi, WBim.t[:, q4, :], uin.t[:, t0:t0 + n], True, True, [WBim, uin], ps_ap=bi.t[:, :n])

            issue_bu(items[0])
            for ii, (ti, t0, n, q4) in enumerate(items):
                if ii + 1 < len(items):
                    issue_bu(items[ii + 1])
                yp = yps[ti % 2]
                br, bi = bur[q4 % 2], bui[q4 % 2]
                sr, si = sre[q4 % 2], sim[q4 % 2]
                wre, wim = wres[q4 % 2], wims[q4 % 2]
                tc, tsn = Tc.t[:, q4, :n], Ts.t[:, q4, :n]
                tt(kb, m1.t[:, :n], br.t[:, :n], tc, ALU.mult, [br, Tc], [m1])
                tt(kb, m2.t[:, :n], bi.t[:, :n], tsn, ALU.mult, [bi, Ts], [m2])
                tt(kb, m3.t[:, :n], bi.t[:, :n], tc, ALU.mult, [bi, Tc], [m3])
                tt(kb, m4.t[:, :n], br.t[:, :n], tsn, ALU.mult, [br, Ts], [m4])
                tt(kb, wre.t[:, :n], m1.t[:, :n], m2.t[:, :n], ALU.add, [m1, m2], [wre])
                tt(kb, wim.t[:, :n], m3.t[:, :n], m4.t[:, :n], ALU.subtract, [m3, m4], [wim])
                for w_, ci_ in ((wre, 0), (wim, 1)):
                    kb.op("dve", lambda e: e.tensor_tensor_scan(
                        out=w_.t[:, :n], data0=Rt.t[:, q4, :n], data1=w_.t[:, :n],
                        initial=car.t[:, q4, ci_:ci_ + 1], op0=ALU.mult, op1=ALU.add), [Rt, w_, car], [w_])
                we_r, we_i = wre.t[:, n - 1:n], wim.t[:, n - 1:n]
                ec, es = Tc.t[:, q4, n - 1:n], Ts.t[:, q4, n - 1:n]
                ts(kb, "dve", ctmp2.t[:, 0:1], we_r, es, None, ALU.mult, None, [wre, Ts], [ctmp2])
                ts(kb, "dve", ctmp.t[:, 0:1], we_i, es, -1.0, ALU.mult, ALU.mult, [wim, Ts], [ctmp])
                stt(kb, car.t[:, q4, 1:2], we_i, ec, ctmp2.t[:, 0:1], ALU.mult, ALU.add, [wim, Tc, ctmp2], [car])
                stt(kb, car.t[:, q4, 0:1], we_r, ec, ctmp.t[:, 0:1], ALU.mult, ALU.add, [wre, Tc, ctmp], [car])
                tt(kb, p1.t[:, :n], wre.t[:, :n], tc, ALU.mult, [wre, Tc], [p1], eng="pool")
                tt(kb, p2.t[:, :n], wim.t[:, :n], tsn, ALU.mult, [wim, Ts], [p2], eng="pool")
                tt(kb, sr.t[:, :n], p1.t[:, :n], p2.t[:, :n], ALU.subtract, [p1, p2], [sr], eng="pool")
                tt(kb, p1.t[:, :n], wre.t[:, :n], tsn, ALU.mult, [wre, Ts], [p1], eng="pool")
                tt(kb, p2.t[:, :n], wim.t[:, :n], tc, ALU.mult, [wim, Tc], [p2], eng="pool")
                tt(kb, si.t[:, :n], p1.t[:, :n], p2.t[:, :n], ALU.add, [p1, p2], [si], eng="pool")
                mm(kb, yp, Cre.t[:, q4, :], sr.t[:, :n], q4 == 0, False, [Cre, sr], ps_ap=yp.t[:, :n])
                mm(kb, yp, Cnim.t[:, q4, :], si.t[:, :n], False, q4 == 3, [Cnim, si], ps_ap=yp.t[:, :n])
                if q4 == 3:
                    kb.op("act", lambda e: e.copy(out=yout.t[:, t0:t0 + n], in_=yp.t[:, :n]), [yp], [yout])

        def lat_view(buf, cm):
            v = buf.t[:, Cx:T]
            if cm:
                return v.rearrange("p (r c) -> p c r", c=GW)
            return v

        for d in range(2):
            coefs(d)
            for k in range(8):
                unit_setup(d, k)
                kb.dma(bufA.t[:], H.t[:, k, :], writes=[bufA])
                ts(kb, "dve", bufB.t[:, 0:Cx], bufA.t[:, 0:Cx], MV.t[:, l, 8 + k, 1:2], MV.t[:, l, k, 1:2],
                   ALU.mult, ALU.add, [bufA, MV], [bufB])
                if col_major:
                    ts(kb, "dve", bufB.t[:, Cx:T].rearrange("p (c r) -> p c r", r=rows), lat_view(bufA, True),
                       MV.t[:, l, 8 + k, 0:1], MV.t[:, l, k, 0:1], ALU.mult, ALU.add, [bufA, MV], [bufB])
                else:
                    ts(kb, "dve", bufB.t[:, Cx:T], bufA.t[:, Cx:T], MV.t[:, l, 8 + k, 0:1], MV.t[:, l, k, 0:1],
                       ALU.mult, ALU.add, [bufA, MV], [bufB])
                if d == 0:
                    unit_scan(bufB, bufA)
                    stt(kb, bufA.t[:], bufB.t[:], vb.t[:, k:k + 1], bufA.t[:], ALU.mult, ALU.add, [bufB, vb, bufA], [bufA])
                    kb.dma(G.t[:, k, :], bufA.t[:], reads=[bufA])
                else:
                    kb.op("dve", lambda e: e.tensor_copy(out=bufC.t[:, 0:Cx], in_=bufB.t[:, 0:Cx][:, ::-1]), [bufB], [bufC])
                    kb.op("dve", lambda e: e.tensor_copy(out=bufC.t[:, Cx:T], in_=bufB.t[:, Cx:T][:, ::-1]), [bufB], [bufC])
                    unit_scan(bufC, bufA)
                    kb.dma(bufB.t[:], G.t[:, k, :], writes=[bufB])
                    tt(kb, bufB.t[:, 0:Cx], bufB.t[:, 0:Cx], bufA.t[:, 0:Cx][:, ::-1], ALU.add, [bufB, bufA], [bufB])
                    tt(kb, bufB.t[:, Cx:T], bufB.t[:, Cx:T], bufA.t[:, Cx:T][:, ::-1], ALU.add, [bufB, bufA], [bufB])
                    act(kb, bufC.t[:, 0:Cx], bufB.t[:, 0:Cx], AF.Gelu, [bufB], [bufC])
                    if col_major:
                        act(kb, lat_view(bufC, True), bufB.t[:, Cx:T].rearrange("p (c r) -> p c r", r=rows), AF.Gelu,
                            [bufB], [bufC])
                    else:
                        act(kb, bufC.t[:, Cx:T], bufB.t[:, Cx:T], AF.Gelu, [bufB], [bufC])
                    kb.dma(G.t[:, k, :], bufC.t[:], reads=[bufC])
            barrier(kb)
    with Phase(kb):
        vb = kb.sb("g_vb", [128, 24])
        wl = kb.sb("g_w", [128, 8, 2048], BF16)
        gb = kb.sb("g_gb", [128, 8, 512], BF16)
        sg = kb.sb("g_sg", [128, 512])
        ob = kb.sb("g_ob", [128, 8, 512])
        Ap = [kb.ps("g_Ap%d" % i) for i in range(2)]
        Bp = [kb.ps("g_Bp%d" % i) for i in range(2)]
        kb.dma(vb.t[:], s5vec[occ], writes=[vb])
        for k in range(8):
            kb.dma(wl.t[:, k, :], wglu[occ, :, k, :], writes=[wl], q="pool")
        for t0 in range(0, T, 512):
            n = min(512, T - t0)
            kb.dma(gb.t[:, :, :n], G.t[:, :, t0:t0 + n], writes=[gb], q="pool")
            for j in range(8):
                A, Bq = Ap[j % 2], Bp[j % 2]
                for k in range(8):
                    mm(kb, A, wl.t[:, k, j * 128:(j + 1) * 128], gb.t[:, k, :n], k == 0, k == 7, [wl, gb], ps_ap=A.t[:, :n])
                for k in range(8):
                    mm(kb, Bq, wl.t[:, k, 1024 + j * 128:1024 + (j + 1) * 128], gb.t[:, k, :n], k == 0, k == 7,
                       [wl, gb], ps_ap=Bq.t[:, :n])
                act(kb, sg.t[:, :n], Bq.t[:, :n], AF.Sigmoid, [Bq, vb], [sg], bias=vb.t[:, 16 + j:17 + j])
                stt(kb, ob.t[:, j, :n], A.t[:, :n], vb.t[:, 8 + j:9 + j], sg.t[:, :n], ALU.add, ALU.mult, [A, vb, sg], [ob])
            kb.dma(Y.t[:, :, t0:t0 + n], ob.t[:, :, :n], reads=[ob])


def host_s5(inp):
    m = {}
    NA = inp["s5_lambda_re"].shape[0]
    def par(a):
        return a.reshape(NA, 2, 32, 2, 64).transpose(0, 1, 3, 4, 2).reshape(NA, 2, 128, 32)
    ldt = np.broadcast_to(inp["s5_log_dt"][..., None], inp["s5_lambda_re"].shape)
    m["s5par"] = np.ascontiguousarray(np.stack([par(inp["s5_lambda_re"]), par(inp["s5_lambda_im"]), par(ldt)], axis=3))
    def masked(a_gpc):
        out = np.zeros((NA, 2, 2, 64, 32, 8, 16), np.float32)
        for g in range(64):
            q, gl = g // 2, g % 2
            out[:, :, gl, :, q, g % 8, :] = a_gpc[:, :, g]
        return out.reshape(NA, 2, 128, 32, 128)
    m["s5bre"] = masked(inp["s5_b_re"])
    m["s5bim"] = masked(inp["s5_b_im"])
    m["s5cre"] = masked(inp["s5_c_re"].transpose(0, 1, 2, 4, 3))
    m["s5cim"] = masked(inp["s5_c_im"].transpose(0, 1, 2, 4, 3))
    m["s5vec"] = np.ascontiguousarray(np.stack(
        [np.concatenate([fm(inp["s5_d"][o]), fm(inp["s5_b_glu"][o])], axis=1) for o in range(NA)], 0))
    m["wglu"] = np.ascontiguousarray(inp["s5_w_glu"].reshape(NA, 8, 128, 2048).transpose(0, 2, 1, 3))
    m["ident"] = np.eye(128, dtype=np.float32)
    return m


E2 = 2048
NE = 16
NV_ML = 48 + 16 + 16 + 16
CH = 128


def phase_mlstm(kb, l, MV, H, Y, XM, XC, Z, QT, KT, VT, GT, HH, ml_wup, ml_wq, ml_wk, ml_wv, ml_wg, ml_bg, mlvec,
                ml_wdown, ident, tri, Lx, Cx):
    T = Cx + Lx
    with Phase(kb):
        vb = kb.sb("a_vb", [128, NV_ML])
        kb.dma(vb.t[:], mlvec, writes=[vb])
        NW = 258
        hb = kb.sb("a_hb", [128, 8, NW])
        ub = kb.sb("a_ub", [128, 8, NW], BF16)
        xmp = kb.sb("a_xmp", [128, NE, NW])
        zb = kb.sb("a_zb", [128, NE, 256])
        xcb = kb.sb("a_xcb", [128, NE, 256])
        acc = kb.sb("a_acc", [128, 256])
        wup = kb.sb("a_wup", [128, 32, 8, 128], BF16)
        for j in range(32):
            kb.dma(wup.t[:, j], ml_wup[j], writes=[wup], q="pool")
        pp = [kb.ps("a_ps%d" % i) for i in range(2)]
        cnt = [0]

        def tile(seq0, seqn, p0, n2, s):
            lo = max(p0 - 1, 0)
            hi = min(p0 + n2 + 1, seqn)
            n1 = hi - lo
            off = lo - (p0 - 1)
            io = p0 - lo
            kb.dma(hb.t[:, :, :n1], H.t[:, :, seq0 + lo:seq0 + hi], writes=[hb])
            if n1 < n2 + 2:
                kb.op("dve", lambda e: e.memset(xmp.t[:], 0.0), writes=[xmp])
            for k in range(8):
                ts(kb, "dve", ub.t[:, k, :n1], hb.t[:, k, :n1], MV.t[:, l, 8 + k, s:s + 1], MV.t[:, l, k, s:s + 1],
                   ALU.mult, ALU.add, [hb, MV], [ub])
            for j in range(32):
                ps = pp[cnt[0] % 2]
                cnt[0] += 1
                for k in range(8):
                    mm(kb, ps, wup.t[:, j, k, :], ub.t[:, k, :n1], k == 0, k == 7, [wup, ub], ps_ap=ps.t[:, :n1])
                if j < NE:
                    kb.op("act", lambda e: e.copy(out=xmp.t[:, j, off:off + n1], in_=ps.t[:, :n1]), [ps], [xmp])
                else:
                    kb.op("act", lambda e: e.copy(out=zb.t[:, j - NE, :n2], in_=ps.t[:, io:io + n2]), [ps], [zb])
            for j in range(NE):
                ts(kb, "dve", acc.t[:, :n2], xmp.t[:, j, 0:n2], vb.t[:, j:j + 1], vb.t[:, 48 + j:49 + j], ALU.mult, ALU.add,
                   [xmp, vb], [acc])
                stt(kb, acc.t[:, :n2], xmp.t[:, j, 1:1 + n2], vb.t[:, 16 + j:17 + j], acc.t[:, :n2], ALU.mult, ALU.add,
                    [xmp, vb, acc], [acc])
                stt(kb, acc.t[:, :n2], xmp.t[:, j, 2:2 + n2], vb.t[:, 32 + j:33 + j], acc.t[:, :n2], ALU.mult, ALU.add,
                    [xmp, vb, acc], [acc])
                act(kb, xcb.t[:, j, :n2], acc.t[:, :n2], AF.Silu, [acc], [xcb])
            a, b_ = seq0 + p0, seq0 + p0 + n2
            kb.dma(XM.t[:, :, a:b_], xmp.t[:, :, 1:1 + n2], reads=[xmp])
            kb.dma(Z.t[:, :, a:b_], zb.t[:, :, :n2], reads=[zb])
            kb.dma(XC.t[:, :, a:b_], xcb.t[:, :, :n2], reads=[xcb])

        for p0 in range(0, Cx, 256):
            tile(0, Cx, p0, min(256, Cx - p0), 1)
        for p0 in range(0, Lx, 256):
            tile(Cx, Lx, p0, 256, 0)
    with Phase(kb):
        xcb = kb.sb("b_xcb", [128, NE, 512], BF16)
        xmb = kb.sb("b_xmb", [128, NE, 512], BF16)
        qkv = [kb.sb("b_qkv%d" % i, [128, NE, 512]) for i in range(3)]
        wch = [kb.sb("b_w%d" % i, [128, NE, 128], BF16) for i in range(2)]
        wg = kb.sb("b_wg", [128, 3, NE, 16])
        bg = kb.sb("b_bg", [16, 1])
        gtb = kb.sb("b_gt", [16, 512])
        pp = [kb.ps("b_ps%d" % i) for i in range(2)]
        gp = kb.ps("b_gp")
        kb.dma(wg.t[:], ml_wg, writes=[wg])
        kb.dma(bg.t[:], ml_bg, writes=[bg])
        cnt = [0]
        for t0 in range(0, T, 512):
            n = min(512, T - t0)
            kb.dma(xcb.t[:, :, :n], XC.t[:, :, t0:t0 + n], writes=[xcb], q="pool")
            kb.dma(xmb.t[:, :, :n], XM.t[:, :, t0:t0 + n], writes=[xmb], q="pool")
            for i, (W, src, dst, scale) in enumerate(((ml_wq, xcb, QT, 1.0), (ml_wk, xcb, KT, 512 ** -0.5), (ml_wv, xmb, VT, 1.0))):
                ob = qkv[i]
                for j in range(NE):
                    w = wch[cnt[0] % 2]
                    ps = pp[cnt[0] % 2]
                    cnt[0] += 1
                    kb.dma(w.t[:], W[j], writes=[w], q="pool")
                    for k in range(NE):
                        mm(kb, ps, w.t[:, k, :], src.t[:, k, :n], k == 0, k == NE - 1, [w, src], ps_ap=ps.t[:, :n])
                    kb.op("act", lambda e: e.mul(ob.t[:, j, :n], ps.t[:, :n], scale), [ps], [ob])
                kb.dma(dst.t[:, :, t0:t0 + n], ob.t[:, :, :n], reads=[ob])
            first = True
            for i in range(3):
                for k in range(NE):
                    mm(kb, gp, wg.t[:, i, k, :], qkv[i].t[:, k, :n], first, (i == 2 and k == NE - 1), [wg, qkv[i]],
                       ps_ap=gp.t[0:16, :n])
                    first = False
            ts(kb, "dve", gtb.t[:, :n], gp.t[0:16, :n], bg.t[:, 0:1], None, ALU.add, None, [gp, bg], [gtb])
            kb.dma(GT.t[:, t0:t0 + n], gtb.t[:, :n], reads=[gtb])
    with Phase(kb):
        idn = kb.sb("m_idn", [128, 128])
        trf = kb.sb("m_trf", [128, 128])
        trb = kb.sb("m_trb", [128, 128])
        ngf = kb.sb("m_ngf", [128, 128])
        ngb = kb.sb("m_ngb", [128, 128])
        ones = kb.sb("m_ones", [128, 128])
        kb.dma(idn.t[:], ident, writes=[idn])
        kb.dma(trf.t[:], tri, writes=[trf])
        kb.dma(trb.t[:], tri.rearrange("a b -> b a"), writes=[trb]) if False else None
        kb.op("dve", lambda e: e.memset(ones.t[:], 1.0), writes=[ones])
        tp = kb.ps("m_tp")
        mm(kb, tp, trf.t[:], idn.t[:], True, True, [trf, idn], ps_ap=tp.t[:, :128])
        kb.op("act", lambda e: e.copy(out=trb.t[:], in_=tp.t[:, :128]), [tp], [trb])
        for tr, ng in ((trf, ngf), (trb, ngb)):
            ts(kb, "dve", ng.t[:], tr.t[:], -1.0, 30000.0, ALU.add, ALU.mult, [tr], [ng])
        qcs = [kb.sb("m_qc%d" % i, [128, 4, CH], BF16) for i in range(2)]
        kcs = [kb.sb("m_kc%d" % i, [128, 4, CH], BF16) for i in range(2)]
        vcs = [kb.sb("m_vc%d" % i, [128, 4, CH], BF16) for i in range(2)]
        gcs = [kb.sb("m_gc%d" % i, [16, CH]) for i in range(2)]
        holds = [kb.sb("m_hold%d" % i, [128, 4, CH]) for i in range(2)]
        idb = kb.sb("m_idb", [128, 128], BF16)
        oneb = kb.sb("m_oneb", [128, 128], BF16)
        kb.op("dve", lambda e: e.tensor_copy(out=idb.t[:], in_=idn.t[:]), [idn], [idb])
        kb.op("dve", lambda e: e.memset(oneb.t[:], 1.0), writes=[oneb])
        qs = kb.sb("m_qs", [128, 4, CH], BF16)
        vtm = kb.sb("m_vtm", [128, 512], BF16)
        ktm = kb.sb("m_ktm", [128, 512], BF16)
        wv = kb.sb("m_wv", [128, 512], BF16)
        cols = kb.sb("m_cols", [128, 16])
        wcb = kb.sb("m_wcb", [128, 2], BF16)
        Lt = kb.sb("m_Lt", [128, 128])
        Em = kb.sb("m_Em", [128, 128])
        DT = kb.sb("m_DT", [128, 128])
        ebm = kb.sb("m_ebm", [128, 128])
        ST = kb.sb("m_ST", [128, 128], BF16)
        rden = kb.sb("m_rden", [128, 128])
        hc = kb.sb("m_hc", [128, 4, CH])
        Cst = kb.sb("m_Cst", [128, 4, 512])
        Csb = kb.sb("m_Csb", [128, 4, 512], BF16)
        nst = kb.sb("m_nst", [128, 4])
        nrep = kb.sb("m_nrep", [128, 4, 128], BF16)
        P0 = kb.ps("m_P0")
        P1 = kb.ps("m_P1")
        P2 = kb.ps("m_P2")
        P3 = kb.ps("m_P3")
        P4 = kb.ps("m_P4")
        P5 = kb.ps("m_P5")
        P6 = kb.ps("m_P6")

        def loads(d, h, c0, sl):
            kb.dma(qcs[sl].t[:], QT.t[:, 4 * h:4 * h + 4, c0:c0 + CH], writes=[qcs[sl]], q="pool")
            kb.dma(kcs[sl].t[:], KT.t[:, 4 * h:4 * h + 4, c0:c0 + CH], writes=[kcs[sl]], q="pool")
            kb.dma(vcs[sl].t[:], VT.t[:, 4 * h:4 * h + 4, c0:c0 + CH], writes=[vcs[sl]], q="pool")
            kb.dma(gcs[sl].t[:], GT.t[:, c0:c0 + CH], writes=[gcs[sl]])
            if d == 1:
                kb.dma(holds[sl].t[:], HH.t[:, 4 * h:4 * h + 4, c0:c0 + CH], writes=[holds[sl]])

        def chunk(d, h, c0, sl):
            tr, ng = (trf, ngf) if d == 0 else (trb, ngb)
            last = CH - 1 if d == 0 else 0
            qc, kc, vc, gc, hold = qcs[sl], kcs[sl], vcs[sl], gcs[sl], holds[sl]
            mm(kb, P0, gc.t[:, :], idn.t[0:16, 0:16], True, True, [gc, idn], ps_ap=P0.t[:, 0:16])
            ic, fc = d * 8 + h, d * 8 + 4 + h
            act(kb, cols.t[:, 8:9], P0.t[:, fc:fc + 1], AF.Exp, [P0], [cols], scale=-1.0)
            act(kb, cols.t[:, 0:1], cols.t[:, 8:9], AF.Ln, [cols], [cols], bias=1.0)
            kb.op("dve", lambda e: e.tensor_copy(out=cols.t[:, 1:2], in_=cols.t[:, 0:1]), [cols], [cols])
            kb.op("dve", lambda e: e.tensor_copy(out=cols.t[:, 2:3], in_=P0.t[:, ic:ic + 1]), [P0], [cols])
            mm(kb, P0, tr.t[:], cols.t[:, 0:2], True, True, [tr, cols], ps_ap=P0.t[:, 16:18])
            ts(kb, "dve", Lt.t[:], ones.t[:], cols.t[:, 0:1], None, ALU.mult, None, [ones, cols], [Lt])
            mm(kb, P1, Lt.t[:], tr.t[:], True, True, [Lt, tr], ps_ap=P1.t[:, :128])
            tt(kb, cols.t[:, 3:4], cols.t[:, 2:3], P0.t[:, 16:17], ALU.add, [cols, P0], [cols])
            ts(kb, "dve", Em.t[:], P1.t[:, :128], -1.0, cols.t[:, 3:4], ALU.mult, ALU.add, [P1, cols], [Em])
            tt(kb, Em.t[:], Em.t[:], tr.t[:], ALU.mult, [Em, tr], [Em])
            tt(kb, Em.t[:], Em.t[:], ng.t[:], ALU.add, [Em, ng], [Em])
            act(kb, DT.t[:], Em.t[:], AF.Exp, [Em], [DT])
            act(kb, ebm.t[:], P1.t[:, :128], AF.Exp, [P1], [ebm], scale=-1.0)
            ts(kb, "dve", cols.t[:, 4:5], P1.t[:, last:last + 1], -1.0, None, ALU.mult, None, [P1], [cols])
            act(kb, cols.t[:, 5:6], cols.t[:, 4:5], AF.Exp, [cols], [cols])
            act(kb, cols.t[:, 6:7], cols.t[:, 3:4], AF.Exp, [cols], [cols], bias=cols.t[:, 4:5])
            kb.op("dve", lambda e: e.tensor_copy(out=wcb.t[:, 0:1], in_=cols.t[:, 6:7]), [cols], [wcb])
            kb.op("dve", lambda e: e.tensor_copy(out=wcb.t[:, 1:2], in_=cols.t[:, 6:7]), [cols], [wcb])
            for dt_ in range(4):
                mm(kb, P2, kc.t[:, dt_, :], qc.t[:, dt_, :], dt_ == 0, dt_ == 3, [kc, qc], ps_ap=P2.t[:, :128])
            tt(kb, ST.t[:], DT.t[:], P2.t[:, :128], ALU.mult, [DT, P2], [ST])
            for dt_ in range(4):
                tt(kb, qs.t[:, dt_, :], qc.t[:, dt_, :], ebm.t[:], ALU.mult, [qc, ebm], [qs])
            for et in range(4):
                mm(kb, P3, vc.t[:, et, :], idb.t[:], True, True, [vc, idb], ps_ap=P3.t[:, et * 128:(et + 1) * 128])
            kb.op("act", lambda e: e.copy(out=vtm.t[:], in_=P3.t[:]), [P3], [vtm])
            for et in range(4):
                mm(kb, P4, kc.t[:, et, :], idb.t[:], True, True, [kc, idb], ps_ap=P4.t[:, et * 128:(et + 1) * 128])
            kb.op("act", lambda e: e.copy(out=ktm.t[:], in_=P4.t[:]), [P4], [ktm])
            mm(kb, P6, oneb.t[:], ST.t[:], True, False, [oneb, ST], ps_ap=P6.t[:, :128])
            for dt_ in range(4):
                mm(kb, P6, nrep.t[:, dt_, :], qs.t[:, dt_, :], False, dt_ == 3, [nrep, qs], ps_ap=P6.t[:, :128])
            ts(kb, "dve", rden.t[:], P6.t[:, :128], -1.0, None, ALU.mult, None, [P6], [rden])
            tt(kb, rden.t[:], rden.t[:], P6.t[:, :128], ALU.max, [rden, P6], [rden])
            ts(kb, "dve", rden.t[:], rden.t[:], 1.0, None, ALU.max, None, [rden], [rden])
            kb.op("dve", lambda e: e.reciprocal(out=rden.t[:], in_=rden.t[:]), [rden], [rden])
            for et in range(4):
                o = P5.t[:, et * 128:(et + 1) * 128]
                mm(kb, P5, vtm.t[:, et * 128:(et + 1) * 128], ST.t[:], True, False, [vtm, ST], ps_ap=o)
                for dt_ in range(4):
                    mm(kb, P5, Csb.t[:, dt_, et * 128:(et + 1) * 128], qs.t[:, dt_, :], False, dt_ == 3, [Csb, qs], ps_ap=o)
                tt(kb, hc.t[:, et, :], rden.t[:], o, ALU.mult, [rden, P5], [hc])
                if d == 1:
                    tt(kb, hc.t[:, et, :], hc.t[:, et, :], hold.t[:, et, :], ALU.add, [hc, hold], [hc])
            kb.dma(HH.t[:, 4 * h:4 * h + 4, c0:c0 + CH], hc.t[:], reads=[hc])
            ts(kb, "dve", wv.t[:], vtm.t[:], cols.t[:, 6:7], None, ALU.mult, None, [vtm, cols], [wv])
            for dt_ in range(4):
                mm(kb, P3, ktm.t[:, dt_ * 128:(dt_ + 1) * 128], wv.t[:], True, True, [ktm, wv], ps_ap=P3.t[:])
                stt(kb, Cst.t[:, dt_, :], Cst.t[:, dt_, :], cols.t[:, 5:6], P3.t[:], ALU.mult, ALU.add, [Cst, cols, P3], [Cst])
                kb.op("act", lambda e: e.copy(out=Csb.t[:, dt_, :], in_=Cst.t[:, dt_, :]), [Cst], [Csb])
                mm(kb, P4, ktm.t[:, dt_ * 128:(dt_ + 1) * 128], wcb.t[:, 0:2], True, True, [ktm, wcb], ps_ap=P4.t[:, 0:2])
                stt(kb, nst.t[:, dt_:dt_ + 1], nst.t[:, dt_:dt_ + 1], cols.t[:, 5:6], P4.t[:, 0:1], ALU.mult, ALU.add,
                    [nst, cols, P4], [nst])
                ts(kb, "dve", nrep.t[:, dt_, :], ones.t[:], nst.t[:, dt_:dt_ + 1], None, ALU.mult, None, [ones, nst], [nrep])

        nc_ctx, nc_all = Cx // CH, T // CH
        for d in range(2):
            units = []
            for h in range(4):
                if d == 0:
                    order = list(range(nc_all))
                else:
                    order = list(range(nc_ctx - 1, -1, -1)) + list(range(nc_all - 1, nc_ctx - 1, -1))
                units += [(h, ci, i == 0) for i, ci in enumerate(order)]
            loads(d, units[0][0], units[0][1] * CH, 0)
            for ui, (h, ci, first) in enumerate(units):
                if ui + 1 < len(units):
                    loads(d, units[ui + 1][0], units[ui + 1][1] * CH, (ui + 1) % 2)
                if first:
                    kb.op("dve", lambda e: e.memset(Cst.t[:], 0.0), writes=[Cst])
                    kb.op("dve", lambda e: e.memset(Csb.t[:], 0.0), writes=[Csb])
                    kb.op("dve", lambda e: e.memset(nst.t[:], 0.0), writes=[nst])
                    kb.op("dve", lambda e: e.memset(nrep.t[:], 0.0), writes=[nrep])
                chunk(d, h, ci * CH, ui % 2)
            barrier(kb)
    with Phase(kb):
        vb = kb.sb("o_vb", [128, NV_ML])
        kb.dma(vb.t[:], mlvec, writes=[vb])
        wd = kb.sb("o_wd", [128, NE, 1024], BF16)
        hmb = kb.sb("o_hmb", [128, NE, 256], BF16)
        for k in range(NE):
            kb.dma(wd.t[:, k, :], ml_wdown[:, k, :], writes=[wd], q="pool")
        hb = kb.sb("o_hb", [128, NE, 256])
        xcb = kb.sb("o_xcb", [128, NE, 256])
        zb = kb.sb("o_zb", [128, NE, 256])
        ob = kb.sb("o_ob", [128, 8, 256])
        ones = kb.sb("o_ones", [128, 128])
        sq = kb.sb("o_sq", [128, 256])
        mean = kb.sb("o_mean", [128, 256])
        m2 = kb.sb("o_m2", [128, 256])
        rstd = kb.sb("o_rstd", [128, 256])
        tmp = kb.sb("o_tmp", [128, 256])
        tmp2 = kb.sb("o_tmp2", [128, 256])
        S1 = kb.ps("o_S1")
        S2 = kb.ps("o_S2")
        pp = [kb.ps("o_ps%d" % i) for i in range(2)]
        kb.op("dve", lambda e: e.memset(ones.t[:], 1.0), writes=[ones])
        for t0 in range(0, T, 256):
            n = min(256, T - t0)
            kb.dma(hb.t[:, :, :n], HH.t[:, :, t0:t0 + n], writes=[hb])
            kb.dma(xcb.t[:, :, :n], XC.t[:, :, t0:t0 + n], writes=[xcb])
            kb.dma(zb.t[:, :, :n], Z.t[:, :, t0:t0 + n], writes=[zb])
            for h in range(4):
                for et in range(4):
                    j = 4 * h + et
                    act(kb, sq.t[:, :n], hb.t[:, j, :n], AF.Square, [hb], [sq])
                    mm(kb, S1, ones.t[:], hb.t[:, j, :n], et == 0, et == 3, [ones, hb], ps_ap=S1.t[:, :n])
                    mm(kb, S2, ones.t[:], sq.t[:, :n], et == 0, et == 3, [ones, sq], ps_ap=S2.t[:, :n])
                kb.op("act", lambda e: e.mul(mean.t[:, :n], S1.t[:, :n], 1.0 / 512), [S1], [mean])
                tt(kb, m2.t[:, :n], mean.t[:, :n], mean.t[:, :n], ALU.mult, [mean], [m2])
                stt(kb, m2.t[:, :n], S2.t[:, :n], 1.0 / 512, m2.t[:, :n], ALU.mult, ALU.subtract, [S2, m2], [m2])
                ts(kb, "dve", m2.t[:, :n], m2.t[:, :n], EPS, None, ALU.add, None, [m2], [m2])
                act(kb, m2.t[:, :n], m2.t[:, :n], AF.Sqrt, [m2], [m2])
                kb.op("dve", lambda e: e.reciprocal(out=rstd.t[:, :n], in_=m2.t[:, :n]), [m2], [rstd])
                for et in range(4):
                    j = 4 * h + et
                    tt(kb, tmp.t[:, :n], hb.t[:, j, :n], mean.t[:, :n], ALU.subtract, [hb, mean], [tmp])
                    tt(kb, tmp.t[:, :n], tmp.t[:, :n], rstd.t[:, :n], ALU.mult, [tmp, rstd], [tmp])
                    ts(kb, "dve", tmp.t[:, :n], tmp.t[:, :n], vb.t[:, 64 + j:65 + j], None, ALU.mult, None, [tmp, vb], [tmp])
                    stt(kb, tmp.t[:, :n], xcb.t[:, j, :n], vb.t[:, 80 + j:81 + j], tmp.t[:, :n], ALU.mult, ALU.add,
                        [xcb, vb, tmp], [tmp])
                    act(kb, tmp2.t[:, :n], zb.t[:, j, :n], AF.Silu, [zb], [tmp2])
                    tt(kb, hmb.t[:, j, :n], tmp.t[:, :n], tmp2.t[:, :n], ALU.mult, [tmp, tmp2], [hmb])
            for j in range(8):
                ps = pp[j % 2]
                for k in range(NE):
                    mm(kb, ps, wd.t[:, k, j * 128:(j + 1) * 128], hmb.t[:, k, :n], k == 0, k == NE - 1, [wd, hmb], ps_ap=ps.t[:, :n])
                kb.op("act", lambda e: e.copy(out=ob.t[:, j, :n], in_=ps.t[:, :n]), [ps], [ob])
            kb.dma(Y.t[:, :, t0:t0 + n], ob.t[:, :, :n], reads=[ob])


def host_ml(inp):
    m = {}
    m["ml_wup"] = np.ascontiguousarray(inp["ml_w_up"][0].reshape(8, 128, 32, 128).transpose(2, 1, 0, 3))
    for nm in ("q", "k", "v"):
        m["ml_w" + nm] = np.ascontiguousarray(inp["ml_w_" + nm][0].reshape(16, 128, 16, 128).transpose(2, 1, 0, 3))
    m["ml_wg"] = np.ascontiguousarray(inp["ml_w_gates"][0].reshape(3, 16, 128, 16).transpose(2, 0, 1, 3))
    m["ml_bg"] = np.ascontiguousarray(inp["ml_b_gates"][0].reshape(16, 1))
    cw = inp["ml_conv_w"][0]
    cols = [fm(cw[0]), fm(cw[1]), fm(cw[2]), fm(inp["ml_conv_b"][0]), fm(inp["ml_gn_g"][0]), fm(inp["ml_skip"][0])]
    m["mlvec"] = np.ascontiguousarray(np.concatenate(cols, axis=1).astype(np.float32))
    m["ml_wdown"] = np.ascontiguousarray(inp["ml_w_down"][0].reshape(16, 128, 1024).transpose(1, 0, 2))
    m["ident"] = np.eye(128, dtype=np.float32)
    m["tri"] = np.triu(np.ones((128, 128), np.float32))
    return m


def build_mega(Lx, Cx, nlayers=DEPTH):
    kb = KB()
    T = Cx + Lx
    cT = kb.din("cT", [128, 8, 2])
    modw = kb.din("modw", [DEPTH, 128, 8, 6144])
    modb = kb.din("modb", [128, DEPTH, 48])
    H0 = kb.dram("hT", [128, 8, T], kind="ExternalInput")
    OUT = kb.dram("oT", [128, 8, T], kind="ExternalOutput")
    p2vec = kb.din("p2vec", [DEPTH, 128, NV_P2])
    wg = kb.din("wg", [DEPTH, NF, 128, 8, 128])
    wu = kb.din("wu", [DEPTH, NF, 128, 8, 128])
    wd = kb.din("wd", [DEPTH, 8, 128, NF, 128])
    s5in = (kb.din("s5par", [2, 2, 128, 3, 32]), kb.din("s5bre", [2, 2, 128, 32, 128]),
            kb.din("s5bim", [2, 2, 128, 32, 128]), kb.din("s5cre", [2, 2, 128, 32, 128]),
            kb.din("s5cim", [2, 2, 128, 32, 128]), kb.din("s5vec", [2, 128, 24]),
            kb.din("wglu", [2, 128, 8, 2048]))
    ident = kb.din("ident", [128, 128])
    tri = kb.din("tri", [128, 128])
    mlin = (kb.din("ml_wup", [32, 128, 8, 128]), kb.din("ml_wq", [16, 128, 16, 128]),
            kb.din("ml_wk", [16, 128, 16, 128]), kb.din("ml_wv", [16, 128, 16, 128]),
            kb.din("ml_wg", [128, 3, 16, 16]), kb.din("ml_bg", [16, 1]), kb.din("mlvec", [128, NV_ML]),
            kb.din("ml_wdown", [128, 16, 1024]))
    cvin = (kb.din("cvvec", [128, NV_CV]), kb.din("cv_win", [128, 8, 2048]), kb.din("cv_wout", [128, 8, 1024]))
    HA = kb.dram("HA", [128, 8, T])
    HB = kb.dram("HB", [128, 8, T])
    Yb = kb.dram("Yb", [128, 8, T])
    Gs = kb.dram("Gs", [128, 8, T])
    big = [kb.dram(nm, [128, 16, T]) for nm in ("XM", "XC", "Z", "QT", "KT", "VT", "HH")]
    GT = kb.dram("GT", [16, T])
    MV = kb.sb("MV", [128, DEPTH, 48, 2])
    phase_mod(kb, MV, cT, modw, modb)
    hin = [H0, HA, HB, HA]
    hout = [HA, HB, HA, OUT]
    for l in range(nlayers):
        kind, occ = l % 3, l // 3
        Hc = hin[l]
        Ho = hout[l] if l < nlayers - 1 else OUT
        if kind == 0:
            phase_s5(kb, l, occ, occ % 2 == 1, MV, Hc, Gs, Yb, *s5in, ident, Lx, Cx)
        elif kind == 1:
            phase_mlstm(kb, l, MV, Hc, Yb, *big[:3], *big[3:6], GT, big[6], *mlin, ident, tri, Lx, Cx)
        else:
            phase_conf(kb, l, MV, Hc, Yb, *cvin, ident, Lx, Cx)
        phase_p2(kb, l, MV, Hc, Yb, Ho, p2vec, wg, wu, wd, Lx, Cx)
    return kb.finish()


def host_maps(inp, nb):
    shared = {}
    shared.update(host_p2(inp))
    shared.update(host_s5(inp))
    shared.update(host_ml(inp))
    shared.update(host_conf(inp))
    maps = []
    for b in range(nb):
        m = dict(shared)
        m.update(host_common(inp, b))
        m["hT"] = to_fm(np.concatenate([inp["ctx"][b], inp["x"][b]], axis=0))
        maps.append(m)
    return maps


def kernel(**inputs):
    inp = {k: np.asarray(v, dtype=np.float32) for k, v in inputs.items()}
    nb, Lx, _ = inp["x"].shape
    Cx = inp["ctx"].shape[1]
    nc = build_mega(Lx, Cx)
    maps = host_maps(inp, nb)
    res = run_bass_kernel_spmd(nc, maps, core_ids=list(range(nb)))
    out = np.stack([from_fm(res.results[b]["oT"])[Cx:] for b in range(nb)], axis=0)
    return out.astype(np.float32)

def build_test(phase, Lx, Cx, l):
    kb = KB()
    T = Cx + Lx
    cT = kb.din("cT", [128, 8, 2])
    modw = kb.din("modw", [DEPTH, 128, 8, 6144])
    modb = kb.din("modb", [128, DEPTH, 48])
    H = kb.dram("hT", [128, 8, T], kind="ExternalInput")
    O = kb.dram("oT", [128, 8, T], kind="ExternalOutput")
    MV = kb.sb("MV", [128, DEPTH, 48, 2])
    phase_mod(kb, MV, cT, modw, modb)
    if phase == "p2":
        Y = kb.dram("yT", [128, 8, T], kind="ExternalInput")
        p2vec = kb.din("p2vec", [DEPTH, 128, NV_P2])
        wg = kb.din("wg", [DEPTH, NF, 128, 8, 128])
        wu = kb.din("wu", [DEPTH, NF, 128, 8, 128])
        wd = kb.din("wd", [DEPTH, 8, 128, NF, 128])
        phase_p2(kb, l, MV, H, Y, O, p2vec, wg, wu, wd, Lx, Cx)
    elif phase == "s5":
        occ = l // 3
        G = kb.dram("Gs", [128, 8, T])
        phase_s5(kb, l, occ, occ % 2 == 1, MV, H, G, O,
                 kb.din("s5par", [2, 2, 128, 3, 32]), kb.din("s5bre", [2, 2, 128, 32, 128]),
                 kb.din("s5bim", [2, 2, 128, 32, 128]), kb.din("s5cre", [2, 2, 128, 32, 128]),
                 kb.din("s5cim", [2, 2, 128, 32, 128]), kb.din("s5vec", [2, 128, 24]),
                 kb.din("wglu", [2, 128, 8, 2048]), kb.din("ident", [128, 128]), Lx, Cx)
    elif phase == "ml":
        XM, XC, Z, QT, KT, VT, HH = [kb.dram(nm, [128, 16, T]) for nm in ("XM", "XC", "Z", "QT", "KT", "VT", "HH")]
        GT = kb.dram("GT", [16, T])
        phase_mlstm(kb, l, MV, H, O, XM, XC, Z, QT, KT, VT, GT, HH,
                    kb.din("ml_wup", [32, 128, 8, 128]), kb.din("ml_wq", [16, 128, 16, 128]),
                    kb.din("ml_wk", [16, 128, 16, 128]), kb.din("ml_wv", [16, 128, 16, 128]),
                    kb.din("ml_wg", [128, 3, 16, 16]), kb.din("ml_bg", [16, 1]), kb.din("mlvec", [128, NV_ML]),
                    kb.din("ml_wdown", [128, 16, 1024]), kb.din("ident", [128, 128]), kb.din("tri", [128, 128]), Lx, Cx)
    elif phase == "conf":
        cvvec = kb.din("cvvec", [128, NV_CV])
        cv_win = kb.din("cv_win", [128, 8, 2048])
        cv_wout = kb.din("cv_wout", [128, 8, 1024])
        phase_conf(kb, l, MV, H, O, cvvec, cv_win, cv_wout, kb.din("ident", [128, 128]), Lx, Cx)
    return kb.finish()


def host_common(inp, b):
    cc = np.stack([inp["c"][b], inp["c_ctx"]], axis=1)
    m = {}
    m["cT"] = np.ascontiguousarray(cc.reshape(8, 128, 2).transpose(1, 0, 2))
    m["modw"] = np.ascontiguousarray(inp["mod_w"].reshape(DEPTH, 8, 128, 6144).transpose(0, 2, 1, 3))
    m["modb"] = np.ascontiguousarray(inp["mod_b"].reshape(DEPTH, 48, 128).transpose(2, 0, 1))
    return m


def host_p2(inp):
    m = {}
    vecs = []
    for l in range(DEPTH):
        cols = [fm(inp["post_ln_g"][l, 0]), fm(inp["post_ln_b"][l, 0]), fm(inp["post_ln_g"][l, 1]),
                fm(inp["post_ln_b"][l, 1]), fm(inp["ffn_conv_b"][l]),
                np.ascontiguousarray(inp["ffn_conv_w"][l].reshape(9, NF, 128).transpose(2, 1, 0)).reshape(128, NF * 9)]
        vecs.append(np.concatenate(cols, axis=1))
    m["p2vec"] = np.ascontiguousarray(np.stack(vecs, 0).astype(np.float32))
    m["wg"] = np.ascontiguousarray(inp["ffn_w_gate"].reshape(DEPTH, 8, 128, NF, 128).transpose(0, 3, 2, 1, 4))
    m["wu"] = np.ascontiguousarray(inp["ffn_w_up"].reshape(DEPTH, 8, 128, NF, 128).transpose(0, 3, 2, 1, 4))
    m["wd"] = np.ascontiguousarray(inp["ffn_w_down"].reshape(DEPTH, NF, 128, 8, 128).transpose(0, 3, 2, 1, 4))
    return m


def host_conf(inp):
    m = {}
    cols = [fm(inp["cv_b_in"][0]),
            np.ascontiguousarray(inp["cv_dw_w"][0].reshape(31, 8, 128).transpose(2, 0, 1)).reshape(128, 248),
            fm(inp["cv_dw_b"][0]), fm(inp["cv_ln_g"][0]), fm(inp["cv_ln_b"][0]), fm(inp["cv_b_out"][0])]
    m["cvvec"] = np.ascontiguousarray(np.concatenate(cols, axis=1).astype(np.float32))
    m["cv_win"] = np.ascontiguousarray(inp["cv_w_in"][0].reshape(8, 128, 2048).transpose(1, 0, 2))
    m["cv_wout"] = np.ascontiguousarray(inp["cv_w_out"][0].reshape(8, 128, 1024).transpose(1, 0, 2))
    m["ident"] = np.eye(128, dtype=np.float32)
    return m
```
